# Optimizing a Trainium2 kernel written in Bass

```python
import math
import jax, jax.numpy as jnp
from jax import lax
import numpy as np

D_MODEL = 1024
BATCH = 4
SEQ = 8192
DEPTH = 1
DEC_BATCH = 2
DEC_SEQ = 8192
PAST_LEN = 128

DA_HEADS = 4
DA_DK = 64
DA_DV = 2 * DA_DK
DA_WIDTH = DA_HEADS * DA_DV
ROT_DIM = DA_DK // 4
ROPE_THETA = 500000.0
Q_BLOCK = 128
GLA_HEADS = 4
GLA_DK = 64
GLA_DV = 128
GLA_QK_WIDTH = GLA_HEADS * GLA_DK
GLA_WIDTH = GLA_HEADS * GLA_DV
GLA_RANK = 16
GLA_TAU = 16.0
GLA_CHUNK = 64
D_FF = 4 * D_MODEL
ALPHA = (2 * DEPTH) ** 0.25
BETA = (8 * DEPTH) ** -0.25
LN_EPS = 1e-5
RMS_EPS = 1e-6
IN_SIZES = (
    DA_HEADS * 2 * DA_DK,
    DA_HEADS * 2 * DA_DK,
    DA_WIDTH,
    GLA_QK_WIDTH,
    GLA_QK_WIDTH,
    GLA_WIDTH,
    GLA_WIDTH,
    GLA_RANK,
    GLA_RANK,
    2 * D_MODEL,
)
D_IN = sum(IN_SIZES)

kernel_name = "hybrid_diffattn_gla_deepnorm_encoder"


def layer_norm(x, g, b):
    xf = x.astype(jnp.float32)
    mu = jnp.mean(xf, axis=-1, keepdims=True)
    var = jnp.mean(jnp.square(xf - mu), axis=-1, keepdims=True)
    y = (xf - mu) * lax.rsqrt(var + LN_EPS) * g.astype(jnp.float32) + b.astype(jnp.float32)
    return y.astype(x.dtype)


def rms_norm(x, g):
    xf = x.astype(jnp.float32)
    y = xf * lax.rsqrt(jnp.mean(jnp.square(xf), axis=-1, keepdims=True) + RMS_EPS)
    return y * g.astype(jnp.float32)


def rope_tables(S):
    inv = 1.0 / (ROPE_THETA ** (jnp.arange(0, ROT_DIM, 2, dtype=jnp.float32) / ROT_DIM))
    ang = jnp.arange(S, dtype=jnp.float32)[:, None] * inv[None, :]
    return jnp.cos(ang), jnp.sin(ang)


def apply_partial_rope(t, cos, sin):
    rot, rest = t[..., :ROT_DIM], t[..., ROT_DIM:]
    x1, x2 = rot[..., :ROT_DIM // 2], rot[..., ROT_DIM // 2:]
    c = cos[None, :, None, None, :].astype(t.dtype)
    s = sin[None, :, None, None, :].astype(t.dtype)
    rot = jnp.concatenate([x1 * c - x2 * s, x2 * c + x1 * s], axis=-1)
    return jnp.concatenate([rot, rest], axis=-1)


def diff_attention(q, k, v, lam, lam_init, subln_g):
    B, S = q.shape[0], q.shape[1]
    nb = S // Q_BLOCK
    qb = q.reshape(B, nb, Q_BLOCK, DA_HEADS, 2, DA_DK).transpose(1, 0, 2, 3, 4, 5)
    scale = DA_DK ** -0.5

    def block(qi):
        s = jnp.einsum('bqhmd,bkhmd->bhmqk', qi, k, preferred_element_type=jnp.float32) * scale
        p = jax.nn.softmax(s, axis=-1)
        a = p[:, :, 0] - lam * p[:, :, 1]
        return jnp.einsum('bhqk,bkhv->bqhv', a.astype(v.dtype), v)

    o = lax.map(block, qb)
    o = o.transpose(1, 0, 2, 3, 4).reshape(B, S, DA_HEADS, DA_DV)
    o = rms_norm(o, subln_g) * (1.0 - lam_init)
    return o.reshape(B, S, DA_WIDTH).astype(v.dtype)


def gla_direction(q, k, v, g):
    B, H, S, DK = q.shape
    DV = v.shape[-1]
    n = S // GLA_CHUNK

    def chunks(t):
        return t.reshape(B, H, n, GLA_CHUNK, t.shape[-1]).transpose(2, 0, 1, 3, 4)

    mask = jnp.tril(jnp.ones((GLA_CHUNK, GLA_CHUNK), dtype=bool))[:, :, None]

    def step(state, inp):
        qc, kc, vc, gc = inp
        b = jnp.cumsum(gc, axis=-2)
        inter = jnp.einsum('bhtk,bhkv->bhtv', qc * jnp.exp(b), state)
        diff = b[:, :, :, None, :] - b[:, :, None, :, :]
        decay = jnp.exp(jnp.where(mask, diff, -jnp.inf))
        attn = jnp.einsum('bhtk,bhsk,bhtsk->bhts', qc, kc, decay)
        intra = jnp.einsum('bhts,bhsv->bhtv', attn, vc)
        b_last = b[:, :, -1:, :]
        state = jnp.exp(b_last[:, :, 0, :])[..., None] * state + jnp.einsum(
            'bhsk,bhsv->bhkv', kc * jnp.exp(b_last - b), vc)
        return state, inter + intra

    state0 = jnp.zeros((B, H, DK, DV), jnp.float32)
    _, o = lax.scan(step, state0, (chunks(q), chunks(k), chunks(v), chunks(g)))
    return o.transpose(1, 2, 0, 3, 4).reshape(B, H, S, DV)


def gla_branch(q, k, v, r, z_f, z_b, w_dec_f, b_dec_f, w_dec_b, b_dec_b, norm_g):
    B, S = q.shape[0], q.shape[1]

    def heads(t, d):
        return t.reshape(B, S, GLA_HEADS, d).transpose(0, 2, 1, 3).astype(jnp.float32)

    g_f = jax.nn.log_sigmoid((z_f @ w_dec_f + b_dec_f).astype(jnp.float32)) / GLA_TAU
    g_b = jax.nn.log_sigmoid((z_b @ w_dec_b + b_dec_b).astype(jnp.float32)) / GLA_TAU
    qh = heads(q, GLA_DK) * (GLA_DK ** -0.5)
    kh = heads(k, GLA_DK)
    vh = heads(v, GLA_DV)
    flip = lambda t: jnp.flip(t, axis=2)
    o_fwd = gla_direction(qh, kh, vh, heads(g_f, GLA_DK))
    o_bwd = flip(gla_direction(flip(qh), flip(kh), flip(vh), flip(heads(g_b, GLA_DK))))
    o = (o_fwd + o_bwd).transpose(0, 2, 1, 3)
    o = rms_norm(o, norm_g) * jax.nn.silu(r.reshape(B, S, GLA_HEADS, GLA_DV).astype(jnp.float32))
    return o.reshape(B, S, GLA_WIDTH).astype(v.dtype)


def encoder_layer(x, l, w_in, lam_q1, lam_k1, lam_q2, lam_k2, subln_g, w_dec_f, b_dec_f,
                  w_dec_b, b_dec_b, gla_norm_g, w_br_a, w_br_b, b_gate, w_out,
                  ln1_g, ln1_b, w_mlp1, w_mlp2, ln2_g, ln2_b):
    B, S, _ = x.shape
    split_idx = np.cumsum(IN_SIZES)[:-1].tolist()
    proj = x @ w_in
    da_q, da_k, da_v, g_q, g_k, g_v, g_r, z_f, z_b, gates = jnp.split(proj, split_idx, axis=-1)

    cos, sin = rope_tables(S)
    q = apply_partial_rope(da_q.reshape(B, S, DA_HEADS, 2, DA_DK), cos, sin)
    k = apply_partial_rope(da_k.reshape(B, S, DA_HEADS, 2, DA_DK), cos, sin)
    v = da_v.reshape(B, S, DA_HEADS, DA_DV)
    lam_init = 0.8 - 0.6 * math.exp(-0.3 * l)
    lam = (jnp.exp(jnp.sum(lam_q1.astype(jnp.float32) * lam_k1.astype(jnp.float32)))
           - jnp.exp(jnp.sum(lam_q2.astype(jnp.float32) * lam_k2.astype(jnp.float32))) + lam_init)
    y_a = diff_attention(q, k, v, lam, lam_init, subln_g)

    y_b = gla_branch(g_q, g_k, g_v, g_r, z_f, z_b, w_dec_f, b_dec_f, w_dec_b, b_dec_b, gla_norm_g)

    gate_a, gate_b = jnp.split(jax.nn.sigmoid(gates + b_gate), 2, axis=-1)
    mix = (gate_a * (y_a @ w_br_a) + gate_b * (y_b @ w_br_b)) @ w_out
    x = layer_norm(ALPHA * x + mix, ln1_g, ln1_b)

    h = jnp.square(jax.nn.relu(x @ w_mlp1)) @ w_mlp2
    return layer_norm(ALPHA * x + h, ln2_g, ln2_b)


def setup_inputs(seed: int = 0) -> dict:
    key = jax.random.key(seed)
    ks = jax.random.split(key, 24)
    f32 = jnp.float32
    nrm = lambda k, shape, s: jax.random.normal(k, shape, f32) * s
    L, D = DEPTH, D_MODEL
    return {
        "x_prompt": nrm(ks[0], (BATCH, SEQ, D), 1.0),
        "x_sample": nrm(ks[1], (DEC_BATCH, DEC_SEQ, D), 1.0),
        "w_in": nrm(ks[2], (L, D, D_IN), D ** -0.5),
        "lam_q1": nrm(ks[3], (L, DA_DK), 0.1),
        "lam_k1": nrm(ks[4], (L, DA_DK), 0.1),
        "lam_q2": nrm(ks[5], (L, DA_DK), 0.1),
        "lam_k2": nrm(ks[6], (L, DA_DK), 0.1),
        "subln_g": 1.0 + nrm(ks[7], (L, DA_DV), 0.02),
        "w_dec_f": nrm(ks[8], (L, GLA_RANK, GLA_QK_WIDTH), GLA_RANK ** -0.5),
        "b_dec_f": nrm(ks[9], (L, GLA_QK_WIDTH), 0.02),
        "w_dec_b": nrm(ks[10], (L, GLA_RANK, GLA_QK_WIDTH), GLA_RANK ** -0.5),
        "b_dec_b": nrm(ks[11], (L, GLA_QK_WIDTH), 0.02),
        "gla_norm_g": 1.0 + nrm(ks[12], (L, GLA_DV), 0.02),
        "w_br_a": nrm(ks[13], (L, DA_WIDTH, D), DA_WIDTH ** -0.5),
        "w_br_b": nrm(ks[14], (L, GLA_WIDTH, D), GLA_WIDTH ** -0.5),
        "b_gate": nrm(ks[15], (L, 2 * D), 0.02),
        "w_out": nrm(ks[16], (L, D, D), BETA * D ** -0.5),
        "ln1_g": 1.0 + nrm(ks[17], (L, D), 0.02),
        "ln1_b": nrm(ks[18], (L, D), 0.02),
        "w_mlp1": nrm(ks[19], (L, D, D_FF), D ** -0.5),
        "w_mlp2": nrm(ks[20], (L, D_FF, D), BETA * D_FF ** -0.5),
        "ln2_g": 1.0 + nrm(ks[21], (L, D), 0.02),
        "ln2_b": nrm(ks[22], (L, D), 0.02),
    }


def reference(x_prompt, x_sample, w_in, lam_q1, lam_k1, lam_q2, lam_k2, subln_g, w_dec_f, b_dec_f,
              w_dec_b, b_dec_b, gla_norm_g, w_br_a, w_br_b, b_gate, w_out,
              ln1_g, ln1_b, w_mlp1, w_mlp2, ln2_g, ln2_b):
    def trunk(x):
        for l in range(DEPTH):
            x = encoder_layer(x, l, w_in[l], lam_q1[l], lam_k1[l], lam_q2[l], lam_k2[l], subln_g[l],
                              w_dec_f[l], b_dec_f[l], w_dec_b[l], b_dec_b[l], gla_norm_g[l],
                              w_br_a[l], w_br_b[l], b_gate[l], w_out[l],
                              ln1_g[l], ln1_b[l], w_mlp1[l], w_mlp2[l], ln2_g[l], ln2_b[l])
        return x

    y_prompt = trunk(x_prompt)
    y_sample = trunk(x_sample)
    return (y_prompt, y_sample)
```

```python
import math
from contextlib import ExitStack

import numpy as np
import concourse.bass as bass
import concourse.mybir as mybir
from concourse.bass_utils import run_bass_kernel_spmd

F32 = mybir.dt.float32
BF16 = mybir.dt.bfloat16
AF = mybir.ActivationFunctionType
ALU = mybir.AluOpType

S = 8192
D = 1024
NBLK = 16
NHEADS_B = 4
ALPHA = 2.0 ** 0.25
LAM_INIT = 0.8 - 0.6 * math.exp(0.0)
LN_EPS = 1e-5
RMS_EPS = 1e-6


class Res:
    __slots__ = ("name", "lw", "lr", "dr", "track")

    def __init__(self, name):
        self.name = name
        self.track = True
        self.lw = []
        self.lr = {}
        self.dr = []


class Op:
    __slots__ = ("eng", "fn", "deps", "dma", "tok", "sig", "sigval")

    def __init__(self, eng, fn, deps, dma, tok):
        self.eng = eng
        self.fn = fn
        self.deps = deps
        self.dma = dma
        self.tok = tok
        self.sig = False
        self.sigval = 0


class Prog:
    ENGS = ("pe", "act", "dve", "pool", "sp")
    ND = 48

    def __init__(self, nc):
        self.nc = nc
        self.ops = {e: [] for e in self.ENGS}
        self.dmas = []
        self.out_toks = []
        self.bar_dma = 0

    def op(self, eng, fn, reads=(), writes=(), dma=False, out=False):
        reads = [r for r in reads if r.track]
        writes = [w for w in writes if w.track]
        deps = set()
        for r in reads:
            deps.update(r.lw)
        wdeps = set()
        join = {}
        for w in writes:
            j = dma and bool(w.lw) and all(t[0] == "d" for t in w.lw) and not w.lr and not w.dr
            join[id(w)] = j
            if not j:
                wdeps.update(w.lw)
            wdeps.update(w.lr.values())
            wdeps.update(w.dr)
        deps |= wdeps
        if dma:
            k = len(self.dmas)
            tok = ("d", k)
            if k >= self.ND:
                deps.add(("d", k - self.ND))
        else:
            tok = ("e", eng, len(self.ops[eng]))
        if eng == "pe" and not dma:
            deps = {d for d in deps if not (d[0] == "e" and d[1] == "pe")}
        o = Op(eng, fn, deps, dma, tok)
        self.ops[eng].append(o)
        if dma:
            self.dmas.append(o)
            if out:
                self.out_toks.append(tok)
        for r in reads:
            if dma:
                r.dr.append(tok)
            else:
                r.lr[eng] = tok
        for w in writes:
            if join[id(w)]:
                w.lw.append(tok)
            else:
                w.lw = [tok]
            w.lr = {}
            w.dr = []
        return tok

    def barrier(self):
        last = {}
        for e in self.ENGS:
            for o in reversed(self.ops[e]):
                if not o.dma and o.fn is not None:
                    last[e] = o.tok
                    break
        dtoks = {("d", k) for k in range(max(self.bar_dma, len(self.dmas) - self.ND), len(self.dmas))}
        self.bar_dma = len(self.dmas)
        for e in self.ENGS:
            deps = set(dtoks) | {t for ee, t in last.items() if ee != e}
            o = Op(e, None, deps, False, ("e", e, len(self.ops[e])))
            self.ops[e].append(o)

    def emit(self):
        nc = self.nc
        fin = Op("sp", None, set(self.out_toks), False, ("e", "sp", len(self.ops["sp"])))
        self.ops["sp"].append(fin)
        for e in self.ENGS:
            for o in self.ops[e]:
                for d in o.deps:
                    if d[0] == "e":
                        self.ops[d[1]][d[2]].sig = True
        for e in self.ENGS:
            c = 0
            for o in self.ops[e]:
                if o.sig and not o.dma and o.fn is not None:
                    c += 1
                o.sigval = c
        esem = {e: nc.alloc_semaphore(name=f"tl_{e}") for e in self.ENGS}
        dsem = [nc.alloc_semaphore(name=f"dq_{i}") for i in range(self.ND)]
        engobj = {"pe": "tensor", "act": "scalar", "dve": "vector", "pool": "gpsimd", "sp": "sync"}
        with nc.Block() as block:
            for e in self.ENGS:
                ops = self.ops[e]

                def body(eng, ops=ops):
                    waited = {}
                    for o in ops:
                        need = {}
                        for d in o.deps:
                            if d[0] == "d":
                                k = d[1]
                                s = dsem[k % self.ND]
                                v = 16 * (k // self.ND + 1)
                            else:
                                s = esem[d[1]]
                                v = self.ops[d[1]][d[2]].sigval
                            if v > need.get(s.num, (None, 0))[1]:
                                need[s.num] = (s, v)
                        for sn, (s, v) in need.items():
                            if waited.get(sn, 0) < v:
                                eng.wait_ge(s, v)
                                waited[sn] = v
                        if o.fn is None:
                            continue
                        ins = o.fn(eng)
                        if o.dma:
                            k = o.tok[1]
                            ins.then_inc(dsem[k % self.ND], 16)
                        elif o.sig:
                            ins.then_inc(esem[o.eng], 1)

                getattr(block, engobj[e])(body)


class T:
    __slots__ = ("t", "r")

    def __init__(self, t, name):
        self.t = t
        self.r = Res(name)

    def __getitem__(self, idx):
        return self.t[idx]


class Ring:
    def __init__(self, items):
        self.items = items
        self.i = 0

    def next(self):
        x = self.items[self.i % len(self.items)]
        self.i += 1
        return x


class KB:
    def __init__(self, debug=()):
        self.nc = bass.Bass("TRN2", target_bir_lowering=False)
        self.P = Prog(self.nc)
        self.debug = set(debug)

    def dram_in(self, name, shape, dt=F32):
        return self.nc.dram_tensor(name, list(shape), dt, kind="ExternalInput").ap()

    def dram_out(self, name, shape, dt=F32):
        return self.nc.dram_tensor(name, list(shape), dt, kind="ExternalOutput").ap()

    def dram_tmp(self, name, shape, dt=F32):
        kind = "ExternalOutput" if name in self.debug else "Internal"
        t = T(self.nc.dram_tensor(name, list(shape), dt, kind=kind).ap(), name)
        t.r.track = False
        return t

    def sb(self, st, name, shape, dt=F32):
        return T(st.enter_context(self.nc.sbuf_tensor(name, list(shape), dt)), name)

    def sbring(self, st, name, shape, dt, n):
        return Ring([self.sb(st, f"{name}{i}", shape, dt) for i in range(n)])

    def mm(self, out, lhsT, rhs, start, stop, reads, writes):
        self.P.op("pe", lambda e: e.matmul(out, lhsT=lhsT, rhs=rhs, start=start, stop=stop, skip_group_check=True),
                  reads=reads, writes=writes)

    def tr(self, out, in_, ident, reads, writes):
        self.P.op("pe", lambda e: e.transpose(out, in_, ident), reads=reads, writes=writes)

    def act(self, out, in_, func, reads, writes, scale=1.0, bias=None, eng="act"):
        if bias is None:
            self.P.op(eng, lambda e: e.activation(out=out, in_=in_, func=func, scale=scale), reads=reads, writes=writes)
        else:
            self.P.op(eng, lambda e: e.activation(out=out, in_=in_, func=func, scale=scale, bias=bias), reads=reads, writes=writes)

    def tt(self, out, in0, in1, op, reads, writes, eng="dve"):
        self.P.op(eng, lambda e: e.tensor_tensor(out=out, in0=in0, in1=in1, op=op), reads=reads, writes=writes)

    def ts(self, out, in0, s1, s2, op0, op1, reads, writes, eng="dve"):
        if op1 is None:
            self.P.op(eng, lambda e: e.tensor_scalar(out=out, in0=in0, scalar1=s1, scalar2=None, op0=op0), reads=reads, writes=writes)
        else:
            self.P.op(eng, lambda e: e.tensor_scalar(out=out, in0=in0, scalar1=s1, scalar2=s2, op0=op0, op1=op1), reads=reads, writes=writes)

    def stt(self, out, in0, scalar, in1, op0, op1, reads, writes):
        self.P.op("dve", lambda e: e.scalar_tensor_tensor(out=out, in0=in0, scalar=scalar, in1=in1, op0=op0, op1=op1),
                  reads=reads, writes=writes)

    def copy(self, out, in_, reads, writes, eng="dve"):
        if eng == "act":
            self.P.op("act", lambda e: e.copy(out=out, in_=in_), reads=reads, writes=writes)
        else:
            self.P.op(eng, lambda e: e.tensor_copy(out=out, in_=in_), reads=reads, writes=writes)

    def recip(self, out, in_, reads, writes):
        self.P.op("dve", lambda e: e.reciprocal(out=out, in_=in_), reads=reads, writes=writes)

    def memset(self, ap, val, writes, eng="dve"):
        self.P.op(eng, lambda e: e.memset(ap, val), writes=writes)

    def dma(self, out, in_, reads=(), writes=(), eng="sp", is_out=False):
        self.P.op(eng, lambda e: e.dma_start(out=out, in_=in_), reads=reads, writes=writes, dma=True, out=is_out)


def build(phases="ABCGDE", debug=()):
    kb = KB(debug)
    nc, P = kb.nc, kb.P
    xT = kb.dram_in("xT", [D, S])
    x = kb.dram_in("x", [S, D])
    w_in = kb.dram_in("w_in", [D, 5152])
    w_rot = kb.dram_in("w_rot", [D, 1024])
    ropeC = kb.dram_in("ropeC", [128, S])
    ropeS = kb.dram_in("ropeS", [128, S])
    lamv = kb.dram_in("lamv", [1, 256])
    subln_g = kb.dram_in("subln_g", [128, 1])
    gnorm_g = kb.dram_in("gnorm_g", [128, 1])
    w_dec_f = kb.dram_in("w_dec_f", [16, 256])
    w_dec_b = kb.dram_in("w_dec_b", [16, 256])
    b_dec_f = kb.dram_in("b_dec_f", [1, 256])
    b_dec_b = kb.dram_in("b_dec_b", [1, 256])
    w_br_a = kb.dram_in("w_br_a", [512, D])
    w_br_b = kb.dram_in("w_br_b", [512, D])
    b_gate = kb.dram_in("b_gate", [128, 16])
    w_out = kb.dram_in("w_out", [D, D])
    ln1_g = kb.dram_in("ln1_g", [128, D])
    ln1_b = kb.dram_in("ln1_b", [128, D])
    ln2_g = kb.dram_in("ln2_g", [128, D])
    ln2_b = kb.dram_in("ln2_b", [128, D])
    w_mlp1 = kb.dram_in("w_mlp1", [D, 4096])
    w_mlp2 = kb.dram_in("w_mlp2", [4096, D])
    cmats = kb.dram_in("cmats", [128, 6, 128])
    cmask = kb.dram_in("cmask", [128, 2, 256])
    y = kb.dram_out("y", [S, D])

    qT_s = kb.dram_tmp("qT_s", [512, S], BF16)
    kT_s = kb.dram_tmp("kT_s", [512, S], BF16)
    v_s = kb.dram_tmp("v_s", [4, 128, 64, 128], BF16)
    gqT_s = kb.dram_tmp("gqT_s", [256, S])
    gkT_s = kb.dram_tmp("gkT_s", [256, S])
    srT_s = kb.dram_tmp("srT_s", [512, S])
    zT_s = kb.dram_tmp("zT_s", [32, S])
    gk_s = kb.dram_tmp("gk_s", [S, 256])
    gv_s = kb.dram_tmp("gv_s", [S, 512], BF16)
    yaT_s = kb.dram_tmp("yaT_s", [512, S], BF16)
    ybT_s = kb.dram_tmp("ybT_s", [512, S], BF16)
    qtT_s = [kb.dram_tmp(f"qtT_s{d}", [256, S], BF16) for d in range(2)]
    ktT_s = [kb.dram_tmp(f"ktT_s{d}", [256, S], BF16) for d in range(2)]
    kh_s = [kb.dram_tmp(f"kh_s{d}", [S, 256], BF16) for d in range(2)]
    obT_s = kb.dram_tmp("obT_s", [512, S])
    x1_s = kb.dram_tmp("x1_s", [S, D])
    xTb_s = kb.dram_tmp("xTb_s", [D, S], BF16)
    w1b_s = kb.dram_tmp("w1b_s", [D, 4096], BF16)
    w2b_s = kb.dram_tmp("w2b_s", [4096, D], BF16)
    wgb_s = kb.dram_tmp("wgb_s", [D, 2048], BF16)
    wab_s = kb.dram_tmp("wab_s", [512, D], BF16)
    wbb_s = kb.dram_tmp("wbb_s", [512, D], BF16)
    wob_s = kb.dram_tmp("wob_s", [D, D], BF16)

    psall = nc.alloc_psum_tensor("psall", [128, 4096], F32)

    class Bank:
        def __init__(self, i):
            self.i = i
            self.r = Res(f"ps{i}")

        def __getitem__(self, idx):
            if not isinstance(idx, tuple):
                idx = (idx, slice(None))
            p, f = idx
            a = 0 if f.start is None else f.start
            b = 512 if f.stop is None else f.stop
            return psall[p, self.i * 512 + a:self.i * 512 + b]

    PS = [Bank(i) for i in range(8)]
    gst = ExitStack()
    cm = kb.sb(gst, "cm", [128, 6, 128], F32)
    msk = kb.sb(gst, "msk", [128, 2, 256], F32)
    ones_r = kb.sb(gst, "ones_r", [1, 128], F32)
    ones_c = kb.sb(gst, "ones_c", [128, 1], F32)
    ones_cb = kb.sb(gst, "ones_cb", [128, 1], BF16)
    lam_t = kb.sb(gst, "lam_t", [1, 8], F32)
    lam_w = kb.sb(gst, "lam_w", [1, 256], F32)
    sg_a = kb.sb(gst, "sg_a", [128, 1], F32)
    sg_b = kb.sb(gst, "sg_b", [128, 1], F32)
    Aall = kb.sb(gst, "Aall", [128, 2, 2, 64], F32)

    kb.dma(cm[:], cmats, writes=[cm.r])
    kb.dma(msk[:], cmask, writes=[msk.r])
    kb.dma(lam_w[:], lamv, writes=[lam_w.r])
    kb.dma(sg_a[:], subln_g, writes=[sg_a.r])
    kb.dma(sg_b[:], gnorm_g, writes=[sg_b.r])
    kb.memset(ones_r[:], 1.0, [ones_r.r])
    kb.memset(ones_c[:], 1.0, [ones_c.r])
    kb.memset(ones_cb[:], 1.0, [ones_cb.r])
    kb.tt(lam_w[:, 0:64], lam_w[:, 0:64], lam_w[:, 64:128], ALU.mult, [lam_w.r], [lam_w.r])
    kb.tt(lam_w[:, 128:192], lam_w[:, 128:192], lam_w[:, 192:256], ALU.mult, [lam_w.r], [lam_w.r])
    P.op("dve", lambda e: e.reduce_sum(out=lam_t[:, 0:1], in_=lam_w[:, 0:64], axis=mybir.AxisListType.X), reads=[lam_w.r], writes=[lam_t.r])
    P.op("dve", lambda e: e.reduce_sum(out=lam_t[:, 1:2], in_=lam_w[:, 128:192], axis=mybir.AxisListType.X), reads=[lam_w.r, lam_t.r], writes=[lam_t.r])
    kb.act(lam_t[:, 0:2], lam_t[:, 0:2], AF.Exp, [lam_t.r], [lam_t.r])
    kb.tt(lam_t[:, 2:3], lam_t[:, 1:2], lam_t[:, 0:1], ALU.subtract, [lam_t.r], [lam_t.r])
    kb.ts(lam_t[:, 2:3], lam_t[:, 2:3], -LAM_INIT, None, ALU.add, None, [lam_t.r], [lam_t.r])
    kb.ts(sg_a[:], sg_a[:], 1.0 - LAM_INIT, None, ALU.mult, None, [sg_a.r], [sg_a.r])
    if "lam" in debug:
        lam_o = kb.dram_out("lam_o", [1, 8])
        kb.dma(lam_o, lam_t[:], reads=[lam_t.r], is_out=True)

    ident = cm[:, 4, :]

    def rms_bcast(Dt, sq, row, psA, psB):
        kb.act(sq[:], Dt[:], AF.Square, [Dt.r], [sq.r])
        kb.mm(psA[0:1, :], ones_c[:], sq[:], True, True, [ones_c.r, sq.r], [psA.r])
        kb.act(row[:], psA[0:1, :], AF.Sqrt, [psA.r], [row.r], scale=1.0 / 128.0, bias=RMS_EPS)
        kb.recip(row[:], row[:], [row.r], [row.r])
        kb.mm(psB[:], ones_r[:], row[:], True, True, [ones_r.r, row.r], [psB.r])

    wjobs = []
    for c in range(8):
        for hf in range(2):
            wjobs.append((w_mlp1[c * 128:(c + 1) * 128, hf * 2048:(hf + 1) * 2048], w1b_s[c * 128:(c + 1) * 128, hf * 2048:(hf + 1) * 2048], None))
    for c in range(16):
        wjobs.append((w_mlp2[c * 256:(c + 1) * 256, :].rearrange("(a p) n -> p a n", p=128),
                      w2b_s[c * 256:(c + 1) * 256, :].rearrange("(a p) n -> p a n", p=128), 2))
    for c in range(8):
        wjobs.append((w_in[c * 128:(c + 1) * 128, 3104:5152], wgb_s[c * 128:(c + 1) * 128, :], None))
    for src_, dst_, nr in ((w_br_a, wab_s, 2), (w_br_b, wbb_s, 2), (w_out, wob_s, 4)):
        for c in range(nr):
            wjobs.append((src_[c * 256:(c + 1) * 256, :].rearrange("(a p) n -> p a n", p=128),
                          dst_[c * 256:(c + 1) * 256, :].rearrange("(a p) n -> p a n", p=128), 2))
    wj_state = {"next": 0, "pend": []}

    def wj_step(wst, wsb, eng, n=1):
        for (j, s_) in wj_state["pend"]:
            src_, dst_, a = wjobs[j]
            b_ = wsb.next()
            bv = b_[:] if a is None else b_[:].rearrange("p (a n) -> p a n", a=a)
            kb.copy(b_[:], s_[:], [s_.r], [b_.r], eng=eng)
            kb.dma(dst_, bv, reads=[b_.r])
        wj_state["pend"] = []
        for _ in range(n):
            j = wj_state["next"]
            if j >= len(wjobs):
                return
            wj_state["next"] = j + 1
            src_, dst_, a = wjobs[j]
            s_ = wst.next()
            sv = s_[:] if a is None else s_[:].rearrange("p (a n) -> p a n", a=a)
            kb.dma(sv, src_, writes=[s_.r])
            wj_state["pend"].append((j, s_))

    if "A" in phases:
        with ExitStack() as st:
            wFc = [kb.sb(st, f"wF{i}", [128, 8, 512], BF16) for i in range(7)]
            wTc = [kb.sb(st, f"wT{i}", [128, 8, 512], BF16) for i in range(3)]
            xb = kb.sbring(st, "xb", [128, 8, 512], BF16, 2)
            rc = kb.sbring(st, "rc", [128, 512], F32, 2)
            rs = kb.sbring(st, "rs", [128, 512], F32, 2)
            tmp = kb.sbring(st, "tmpA", [128, 512], F32, 4)
            stb = kb.sbring(st, "stbA", [128, 512], BF16, 4)
            stf = kb.sbring(st, "stfA", [128, 512], F32, 4)
            psr = Ring(PS)

            xs = kb.sbring(st, "xsA", [128, 8, 512], F32, 2)
            cvt_i = [0]

            def wload(dst, src, s0, n):
                s_ = xs.next()
                kb.dma(s_[:, :, 0:n], src[:, s0:s0 + n].rearrange("(c p) n -> p c n", p=128), writes=[s_.r])
                kb.copy(dst[:, :, 0:n], s_[:, :, 0:n], [s_.r], [dst.r], eng=("dve", "act", "pool")[cvt_i[0] % 3])
                cvt_i[0] += 1

            wload(wFc[0], w_in, 0, 512)
            wload(wFc[1], w_rot, 0, 512)
            wload(wFc[2], w_in, 512, 512)
            wload(wFc[3], w_rot, 512, 512)
            wload(wFc[4], w_in, 1536, 512)
            wload(wFc[5], w_in, 2560, 512)
            wload(wFc[6], w_in, 3072, 32)
            wload(wTc[0], w_in, 1024, 512)
            wload(wTc[1], w_in, 2048, 512)
            wload(wTc[2], w_in, 1792, 256)

            def fm(xt, j, ps, m=128):
                w_ = wFc[j // 4]
                o = (j % 4) * 128
                for kc in range(8):
                    kb.mm(ps[0:m, :], w_[:, kc, o:o + m], xt[:, kc, :], kc == 0, kc == 7, [w_.r, xt.r], [ps.r])

            def a_dma(tb):
                t0 = tb * 512
                xt, xs_ = xb.next(), xs.next()
                kb.dma(xs_[:], xT[:, t0:t0 + 512].rearrange("(c p) n -> p c n", p=128), writes=[xs_.r])
                c_t, s_t = rc.next(), rs.next()
                kb.dma(c_t[:], ropeC[:, t0:t0 + 512], writes=[c_t.r])
                kb.dma(s_t[:], ropeS[:, t0:t0 + 512], writes=[s_t.r])
                return xt, c_t, s_t, xs_, t0

            def a_cvt(ld):
                xt, c_t, s_t, xs_, t0 = ld
                kb.copy(xt[:, 0:4, :], xs_[:, 0:4, :], [xs_.r], [xt.r], eng="dve")
                kb.copy(xt[:, 4:8, :], xs_[:, 4:8, :], [xs_.r], [xt.r], eng="pool")
                kb.dma(xTb_s[:, t0:t0 + 512].rearrange("(c p) n -> p c n", p=128), xt[:], reads=[xt.r])

            nxtA = a_dma(0)
            a_cvt(nxtA)
            for tb in range(NBLK):
                t0 = tb * 512
                xt, c_t, s_t = nxtA[0:3]
                if tb + 1 < NBLK:
                    nxtA = a_dma(tb + 1)
                for base, dst in ((0, qT_s), (8, kT_s)):
                    for h in range(4):
                        p1, p2 = psr.next(), psr.next()
                        fm(xt, base + h, p1)
                        fm(xt, base + 4 + h, p2)
                        a1, a2, ob = tmp.next(), tmp.next(), stb.next()
                        kb.tt(a1[:], p1[:], c_t[:], ALU.mult, [p1.r, c_t.r], [a1.r])
                        kb.tt(a2[:], p2[:], s_t[:], ALU.mult, [p2.r, s_t.r], [a2.r])
                        kb.tt(ob[:], a1[:], a2[:], ALU.add, [a1.r, a2.r], [ob.r])
                        kb.dma(dst[h * 128:(h + 1) * 128, t0:t0 + 512], ob[:], reads=[ob.r], writes=[dst.r])
                for j, dst in ((16, gqT_s), (17, gqT_s), (18, gkT_s), (19, gkT_s)):
                    p1 = psr.next()
                    fm(xt, j, p1)
                    of = stf.next()
                    kb.copy(of[:], p1[:], [p1.r], [of.r], eng="act")
                    r0 = (j % 2) * 128
                    kb.dma(dst[r0:r0 + 128, t0:t0 + 512], of[:], reads=[of.r], writes=[dst.r])
                if tb + 1 < NBLK:
                    a_cvt(nxtA)
                for h in range(4):
                    p1 = psr.next()
                    fm(xt, 20 + h, p1)
                    of = stf.next()
                    kb.act(of[:], p1[:], AF.Silu, [p1.r], [of.r])
                    kb.dma(srT_s[h * 128:(h + 1) * 128, t0:t0 + 512], of[:], reads=[of.r], writes=[srT_s.r])
                p1 = psr.next()
                fm(xt, 24, p1, m=32)
                of = stf.next()
                kb.copy(of[0:32, :], p1[0:32, :], [p1.r], [of.r], eng="act")
                kb.dma(zT_s[:, t0:t0 + 512], of[0:32, :], reads=[of.r], writes=[zT_s.r])
                for ts_ in range(4):
                    n = tb * 4 + ts_
                    for grp in range(3):
                        ncol = 256 if grp == 2 else 512
                        p1 = psr.next()
                        for kc in range(8):
                            kb.mm(p1[:, 0:ncol], xt[:, kc, ts_ * 128:(ts_ + 1) * 128], wTc[grp][:, kc, 0:ncol],
                                  kc == 0, kc == 7, [wTc[grp].r, xt.r], [p1.r])
                        if grp == 0:
                            ob = stb.next()
                            kb.copy(ob[:], p1[:], [p1.r], [ob.r], eng="act")
                            kb.dma(v_s[:, :, n, :].rearrange("h p d -> p h d"), ob[:].rearrange("p (h d) -> p h d", h=4),
                                   reads=[ob.r], writes=[v_s.r])
                        elif grp == 1:
                            ob = stb.next()
                            kb.copy(ob[:], p1[:], [p1.r], [ob.r], eng="dve")
                            kb.dma(gv_s[n * 128:(n + 1) * 128, :], ob[:], reads=[ob.r], writes=[gv_s.r])
                        else:
                            of = stf.next()
                            kb.copy(of[:, 0:256], p1[:, 0:256], [p1.r], [of.r], eng="act")
                            kb.dma(gk_s[n * 128:(n + 1) * 128, :], of[:, 0:256], reads=[of.r], writes=[gk_s.r])
            P.barrier()

    if "B" in phases:
        with ExitStack() as st:
            KT = kb.sbring(st, "KT", [128, S], BF16, 2)
            Q0 = kb.sbring(st, "Q0p", [128, S], BF16, 2)
            Q1 = kb.sbring(st, "Q1p", [128, S], BF16, 2)
            VV = kb.sbring(st, "VV", [128, 64, 128], BF16, 2)
            pt = kb.sbring(st, "ptB", [128, 2, 512], BF16, 8)
            accD = [kb.sbring(st, f"accD{p}", [128, 512], F32, 2) for p in range(2)]
            accD1 = kb.sbring(st, "accM1_", [128, 512], F32, 2)
            f1 = kb.sbring(st, "f1B", [128, 512], F32, 8)
            rw = kb.sbring(st, "rwB", [1, 512], F32, 4)
            yo = kb.sbring(st, "yoB", [128, 512], BF16, 2)
            ones_bb = kb.sb(st, "ones_bb", [128, 128], BF16)
            ones_ff = kb.sb(st, "ones_ff", [128, 128], F32)
            nlam = kb.sb(st, "nlam", [128, 1], F32)
            Spair = Ring([(PS[0], PS[1]), (PS[2], PS[3])])
            O = [PS[4], PS[5]]
            E0, L1 = PS[6], PS[7]
            kb.memset(ones_bb[:], 1.0, [ones_bb.r])
            kb.memset(ones_ff[:], 1.0, [ones_ff.r])
            for q_ in Q0.items:
                kb.memset(q_[64:128, :], 0.0, [q_.r], eng="pool")
            for q_ in Q1.items:
                kb.memset(q_[0:64, :], 0.0, [q_.r], eng="pool")
            kb.mm(E0[:, 0:1], ones_r[:], lam_t[:, 2:3], True, True, [ones_r.r, lam_t.r], [E0.r])
            kb.copy(nlam[:], E0[:, 0:1], [E0.r], [nlam.r])

            def load_head(h):
                k_, q0_, q1_, v_ = KT.next(), Q0.next(), Q1.next(), VV.next()
                for c in range(4):
                    cs = slice(c * 2048, (c + 1) * 2048)
                    kb.dma(k_[:, cs], kT_s[h * 128:(h + 1) * 128, cs], writes=[k_.r])
                    kb.dma(q0_[0:64, cs], qT_s[h * 128:h * 128 + 64, cs], writes=[q0_.r])
                    kb.dma(q1_[64:128, cs], qT_s[h * 128 + 64:(h + 1) * 128, cs], writes=[q1_.r])
                    kb.dma(v_[:, c * 16:(c + 1) * 16, :], v_s[h, :, c * 16:(c + 1) * 16, :], writes=[v_.r])
                return k_, (q0_, q1_), v_

            def qk_exp(k_, q_, q0, kt):
                sa, sb_ = Spair.next()
                for m, sp in enumerate((sa, sb_)):
                    kb.mm(sp[:], k_[:, kt * 128:(kt + 1) * 128], q_[m][:, q0:q0 + 512], True, True, [k_.r, q_[m].r], [sp.r])
                p_ = pt.next()
                kb.act(p_[:].rearrange("p m q -> p (m q)"), psall[:, sa.i * 512:sa.i * 512 + 1024], AF.Exp, [sa.r, sb_.r], [p_.r], scale=0.125)
                return p_

            def epilogue(h, q0, aD):
                o0, o1, l1s = f1.next(), f1.next(), f1.next()
                kb.copy(o0[:], O[0][:], [O[0].r], [o0.r])
                kb.copy(o1[:], O[1][:], [O[1].r], [o1.r])
                kb.copy(l1s[:], L1[:], [L1.r], [l1s.r])
                kb.tt(aD[0][:], aD[0][:], aD[1][:], ALU.add, [aD[0].r, aD[1].r], [aD[0].r], eng="pool")
                yield
                kb.mm(E0[:], ones_ff[:], aD[2][:], True, True, [ones_ff.r, aD[2].r], [E0.r])
                yield
                kb.tt(l1s[:], l1s[:], E0[:], ALU.add, [l1s.r, E0.r], [l1s.r])
                kb.recip(l1s[:], l1s[:], [l1s.r], [l1s.r])
                kb.stt(o1[:], o1[:], nlam[:], l1s[:], ALU.mult, ALU.mult, [o1.r, nlam.r, l1s.r], [o1.r])
                yield
                kb.mm(E0[:], ones_ff[:], aD[0][:], True, True, [ones_ff.r, aD[0].r], [E0.r])
                yield
                r0 = f1.next()
                kb.recip(r0[:], E0[:], [E0.r], [r0.r])
                kb.tt(o0[:], o0[:], r0[:], ALU.mult, [o0.r, r0.r], [o0.r])
                kb.tt(o0[:], o0[:], o1[:], ALU.add, [o0.r, o1.r], [o0.r])
                yield
                sq = f1.next()
                kb.tt(sq[:], o0[:], o0[:], ALU.mult, [o0.r], [sq.r], eng="pool")
                yield
                kb.mm(E0[0:1, :], ones_c[:], sq[:], True, True, [ones_c.r, sq.r], [E0.r])
                yield
                row = rw.next()
                kb.act(row[:], E0[0:1, :], AF.Ln, [E0.r], [row.r], scale=1.0 / 128.0, bias=RMS_EPS)
                kb.act(row[:], row[:], AF.Exp, [row.r], [row.r], scale=-0.5)
                yield
                kb.mm(E0[:], ones_r[:], row[:], True, True, [ones_r.r, row.r], [E0.r])
                yield
                yt = yo.next()
                kb.stt(yt[:], o0[:], sg_a[:], E0[:], ALU.mult, ALU.mult, [o0.r, sg_a.r, E0.r], [yt.r])
                kb.dma(yaT_s[h * 128:(h + 1) * 128, q0:q0 + 512], yt[:], reads=[yt.r])

            pend = []

            def pump():
                for g in list(pend):
                    try:
                        next(g)
                    except StopIteration:
                        pend.remove(g)

            nxt = load_head(0)
            for h in range(NHEADS_B):
                k_, q_, v_ = nxt
                if h < 3:
                    nxt = load_head(h + 1)
                steps = [(qb, kt) for qb in range(NBLK) for kt in range(64)]
                ns = len(steps)
                ptile = {}
                accs = {}

                def consume(sj):
                    qb, kt = steps[sj]
                    p_cur = ptile.pop(sj)
                    if kt == 0:
                        accs[qb] = [accD[0].next(), accD[1].next(), accD1.next()]
                    aD = accs[qb]
                    for m in range(2):
                        kb.mm(O[m][:], v_[:, kt, :], p_cur[:, m, :], kt == 0, kt == 63, [v_.r, p_cur.r], [O[m].r])
                    if kt % 3 == 2:
                        if kt == 2:
                            kb.copy(aD[2][:], p_cur[:, 1, :], [p_cur.r], [aD[2].r])
                        else:
                            kb.tt(aD[2][:], aD[2][:], p_cur[:, 1, :], ALU.add, [aD[2].r, p_cur.r], [aD[2].r])
                    else:
                        kb.mm(L1[:], ones_bb[:], p_cur[:, 1, :], kt == 0, kt == 63, [ones_bb.r, p_cur.r], [L1.r])
                    par = kt % 2
                    if kt < 2:
                        kb.copy(aD[par][:], p_cur[:, 0, :], [p_cur.r], [aD[par].r])
                    else:
                        kb.tt(aD[par][:], aD[par][:], p_cur[:, 0, :], ALU.add, [aD[par].r, p_cur.r], [aD[par].r])
                    if kt % 4 == 3:
                        pump()
                    if kt == 63:
                        g = epilogue(h, qb * 512, accs.pop(qb))
                        next(g)
                        pend.append(g)

                ptile[0] = qk_exp(k_, q_, 0, 0)
                for si in range(ns):
                    if si + 1 < ns:
                        nqb, nkt = steps[si + 1]
                        ptile[si + 1] = qk_exp(k_, q_, nqb * 512, nkt)
                    if si >= 1:
                        consume(si - 1)
                consume(ns - 1)
            while pend:
                pump()
            P.barrier()

    if "C" in phases:
        with ExitStack() as st:
            Wd = kb.sb(st, "Wd", [33, 512], F32)
            za = kb.sbring(st, "za", [33, 512], F32, 2)
            gqb = kb.sbring(st, "gqC", [128, 2, 512], F32, 2)
            gkb = kb.sbring(st, "gkC", [128, 2, 512], F32, 2)
            gktb = kb.sbring(st, "gktC", [128, 4, 256], F32, 2)
            spt = kb.sbring(st, "spt", [128, 512], F32, 3)
            et = kb.sbring(st, "etC", [128, 512], F32, 3)
            eqk = kb.sbring(st, "eqk", [128, 4, 128], F32, 3)
            ekk = kb.sbring(st, "ekk", [128, 4, 128], F32, 3)
            ekh = kb.sbring(st, "ekh", [128, 512], F32, 3)
            oq = [kb.sbring(st, f"oqC{d}", [128, 2, 512], BF16, 2) for d in range(2)]
            ok_ = [kb.sbring(st, f"okC{d}", [128, 2, 512], BF16, 2) for d in range(2)]
            oh = [kb.sbring(st, f"ohC{d}", [128, 4, 256], BF16, 2) for d in range(2)]
            wst1 = kb.sbring(st, "wstC1", [128, 2048], F32, 4)
            wsb1 = kb.sbring(st, "wsbC1", [128, 2048], BF16, 3)
            psr = Ring(PS)
            kb.memset(Wd[:], 0.0, [Wd.r])
            kb.dma(Wd[0:16, 0:256], w_dec_f, writes=[Wd.r])
            kb.dma(Wd[16:32, 256:512], w_dec_b, writes=[Wd.r])
            kb.dma(Wd[32:33, 0:256], b_dec_f, writes=[Wd.r])
            kb.dma(Wd[32:33, 256:512], b_dec_b, writes=[Wd.r])
            for z_ in za.items:
                kb.memset(z_[32:33, :], 1.0, [z_.r])

            def c1_loads(tb):
                t0 = tb * 512
                z_, gq_, gk_, gkt_ = za.next(), gqb.next(), gkb.next(), gktb.next()
                kb.dma(z_[0:32, :], zT_s[:, t0:t0 + 512], writes=[z_.r])
                kb.dma(gq_[:], gqT_s[:, t0:t0 + 512].rearrange("(h p) t -> p h t", p=128), writes=[gq_.r])
                kb.dma(gk_[:], gkT_s[:, t0:t0 + 512].rearrange("(h p) t -> p h t", p=128), writes=[gk_.r])
                kb.dma(gkt_[:], gk_s[t0:t0 + 512, :].rearrange("(n p) c -> p n c", p=128), writes=[gkt_.r])
                return z_, gq_, gk_, gkt_

            nxt = c1_loads(0)
            for tb in range(NBLK):
                t0 = tb * 512
                z_, gq_, gk_, gkt_ = nxt
                if tb + 1 < NBLK:
                    nxt = c1_loads(tb + 1)
                wj_step(wst1, wsb1, "pool", n=1)
                oq_ = [oq[d].next() for d in range(2)]
                okk = [ok_[d].next() for d in range(2)]
                ohh = [oh[d].next() for d in range(2)]
                for ti in range(4):
                    n = tb * 4 + ti
                    cs = slice(ti * 128, (ti + 1) * 128)
                    pu = psr.next()
                    kb.mm(pu[:], z_[:, cs], Wd[:], True, True, [z_.r, Wd.r], [pu.r])
                    e_, sp_ = et.next(), spt.next()
                    kb.act(e_[:], pu[:], AF.Exp, [pu.r], [e_.r], scale=-1.0)
                    kb.act(sp_[:], e_[:], AF.Ln, [e_.r], [sp_.r], bias=1.0)
                    pf, pt_ = psr.next(), psr.next()
                    for d in range(2):
                        for hp in range(2):
                            i4 = d * 2 + hp
                            kb.mm(pf[:, i4 * 128:(i4 + 1) * 128], sp_[:, d * 256 + hp * 128:d * 256 + (hp + 1) * 128], cm[:, d, :],
                                  True, True, [sp_.r, cm.r], [pf.r])
                        kb.mm(pt_[:, d * 256:(d + 1) * 256], cm[:, 2 + d, :], sp_[:, d * 256:(d + 1) * 256], True, True, [sp_.r, cm.r], [pt_.r])
                    eq_, ek_, eh_ = eqk.next(), ekk.next(), ekh.next()
                    kb.act(eq_[:].rearrange("p a t -> p (a t)"), pf[:], AF.Exp, [pf.r], [eq_.r], scale=-1.0 / 16.0)
                    kb.act(ek_[:].rearrange("p a t -> p (a t)"), pf[:], AF.Exp, [pf.r], [ek_.r], scale=1.0 / 16.0)
                    kb.act(eh_[:], pt_[:], AF.Exp, [pt_.r], [eh_.r], scale=-1.0 / 16.0)
                    kb.copy(Aall[:, 0, :, n:n + 1], eq_[:, 0:2, 127:128], [eq_.r], [Aall.r], eng="pool")
                    kb.copy(Aall[:, 1, :, n:n + 1], eq_[:, 2:4, 0:1], [eq_.r], [Aall.r], eng="pool")
                    for d in range(2):
                        kb.stt(oq_[d][:, :, cs], gq_[:, :, cs], 0.125, eq_[:, 2 * d:2 * d + 2, :], ALU.mult, ALU.mult,
                               [gq_.r, eq_.r], [oq_[d].r])
                        kb.tt(okk[d][:, :, cs], gk_[:, :, cs], ek_[:, 2 * d:2 * d + 2, :], ALU.mult, [gk_.r, ek_.r], [okk[d].r])
                        kb.tt(ohh[d][:, ti, :], gkt_[:, ti, :], eh_[:, d * 256:(d + 1) * 256], ALU.mult, [gkt_.r, eh_.r], [ohh[d].r])
                for d in range(2):
                    kb.dma(qtT_s[d][:, t0:t0 + 512].rearrange("(h p) t -> p h t", p=128), oq_[d][:], reads=[oq_[d].r])
                    kb.dma(ktT_s[d][:, t0:t0 + 512].rearrange("(h p) t -> p h t", p=128), okk[d][:], reads=[okk[d].r])
                    kb.dma(kh_s[d][t0:t0 + 512, :].rearrange("(n p) c -> p n c", p=128), ohh[d][:], reads=[ohh[d].r])
            wj_step(wst1, wsb1, "pool", n=0)
            P.barrier()

        if "G" in phases:
            with ExitStack() as st:
                qt = kb.sbring(st, "qtC", [128, 2, 512], BF16, 2)
                kt_ = kb.sbring(st, "ktC", [128, 2, 512], BF16, 2)
                kh = kb.sbring(st, "khC", [128, 4, 256], BF16, 2)
                vv = kb.sbring(st, "vvC", [128, 4, 512], BF16, 2)
                obl = kb.sbring(st, "oblC", [128, 4, 512], F32, 2)
                srl = kb.sbring(st, "srlC", [128, 4, 512], F32, 3)
                attb = kb.sbring(st, "attb", [128, 2, 128], BF16, 4)
                STf = [kb.sb(st, f"STf{hp}", [128, 256], F32) for hp in range(2)]
                STb = [kb.sbring(st, f"STb{hp}", [128, 256], BF16, 3) for hp in range(2)]
                f1 = kb.sbring(st, "f1C", [128, 512], F32, 16)
                rw = kb.sbring(st, "rwC", [1, 512], F32, 4)
                yo = kb.sbring(st, "yoC", [128, 512], BF16, 4)
                wst2 = kb.sbring(st, "wstC2", [128, 2048], F32, 4)
                wsb2 = kb.sbring(st, "wsbC2", [128, 2048], BF16, 3)
                Obank = PS[0:4]
                psr = Ring(PS[4:7])
                Ebank = PS[7]
                ones_ff2 = kb.sb(st, "ones_ff2", [128, 128], F32)
                kb.memset(ones_ff2[:], 1.0, [ones_ff2.r])

                def c2_loads(d, tb):
                    t0 = tb * 512
                    q_, k_, h_, v_ = qt.next(), kt_.next(), kh.next(), vv.next()
                    kb.dma(q_[:], qtT_s[d][:, t0:t0 + 512].rearrange("(h p) t -> p h t", p=128), writes=[q_.r])
                    kb.dma(k_[:], ktT_s[d][:, t0:t0 + 512].rearrange("(h p) t -> p h t", p=128), writes=[k_.r])
                    kb.dma(h_[:], kh_s[d][t0:t0 + 512, :].rearrange("(n p) c -> p n c", p=128), writes=[h_.r])
                    kb.dma(v_[:], gv_s[t0:t0 + 512, :].rearrange("(n p) c -> p n c", p=128), writes=[v_.r])
                    ol = sl = None
                    if d == 0:
                        ol, sl = obl.next(), srl.next()
                        kb.dma(ol[:], obT_s[:, t0:t0 + 512].rearrange("(h p) t -> p h t", p=128), writes=[ol.r])
                        kb.dma(sl[:], srT_s[:, t0:t0 + 512].rearrange("(h p) t -> p h t", p=128), writes=[sl.r])
                    return q_, k_, h_, v_, ol, sl

                def c2_epilogue(t0, ol, sl):
                    Dts = []
                    for hd in range(4):
                        Dt, sq = f1.next(), f1.next()
                        kb.tt(Dt[:], Obank[hd][:], ol[:, hd, :], ALU.add, [Obank[hd].r, ol.r], [Dt.r])
                        kb.tt(sq[:], Dt[:], Dt[:], ALU.mult, [Dt.r], [sq.r], eng="pool")
                        Dts.append((Dt, sq))
                    yield
                    for hd in range(4):
                        Dt, sq = Dts[hd]
                        kb.mm(Ebank[:], ones_ff2[:], sq[:], True, True, [ones_ff2.r, sq.r], [Ebank.r])
                        yield
                        kb.act(sq[:], Ebank[:], AF.Ln, [Ebank.r], [sq.r], scale=1.0 / 128.0, bias=RMS_EPS)
                        kb.act(sq[:], sq[:], AF.Exp, [sq.r], [sq.r], scale=-0.5)
                        yield
                        kb.stt(Dt[:], Dt[:], sg_b[:], sq[:], ALU.mult, ALU.mult, [Dt.r, sg_b.r, sq.r], [Dt.r])
                        yt = yo.next()
                        kb.tt(yt[:], Dt[:], sl[:, hd, :], ALU.mult, [Dt.r, sl.r], [yt.r], eng="pool")
                        kb.dma(ybT_s[hd * 128:(hd + 1) * 128, t0:t0 + 512], yt[:], reads=[yt.r])

                pend2 = []

                def pump2():
                    for g in list(pend2):
                        try:
                            next(g)
                        except StopIteration:
                            pend2.remove(g)

                for d in (1, 0):
                    if d == 0:
                        P.barrier()
                    cur = []
                    for hp in range(2):
                        kb.memset(STf[hp][:], 0.0, [STf[hp].r])
                        sb_ = STb[hp].next()
                        kb.memset(sb_[:], 0.0, [sb_.r])
                        cur.append(sb_)
                    blocks = list(range(NBLK)) if d == 0 else list(range(NBLK - 1, -1, -1))
                    nxt = c2_loads(d, blocks[0])
                    for bi, tb in enumerate(blocks):
                        t0 = tb * 512
                        q_, k_, h_, v_, ol, sl = nxt
                        if bi + 1 < NBLK:
                            nxt = c2_loads(d, blocks[bi + 1])
                        wj_step(wst2, wsb2, "act", n=1)
                        tiles = range(4) if d == 0 else range(3, -1, -1)
                        for ti in tiles:
                            c0 = ti * 128
                            n = tb * 4 + ti
                            abs_ = []
                            for hp in range(2):
                                ab = attb.next()
                                for hh in range(2):
                                    pa = psr.next()
                                    kb.mm(pa[:, 0:128], k_[hh * 64:(hh + 1) * 64, hp, c0:c0 + 128],
                                          q_[hh * 64:(hh + 1) * 64, hp, c0:c0 + 128], True, True, [k_.r, q_.r], [pa.r])
                                    kb.tt(ab[:, hh, :], pa[:, 0:128], msk[:, d, 0:128], ALU.mult, [pa.r, msk.r], [ab.r])
                                abs_.append(ab)
                            for hp in range(2):
                                for hh in range(2):
                                    hd = hp * 2 + hh
                                    kb.mm(Obank[hd][:, c0:c0 + 128], v_[:, ti, hd * 128:(hd + 1) * 128], abs_[hp][:, hh, :], True, False,
                                          [v_.r, abs_[hp].r], [Obank[hd].r])
                            for hp in range(2):
                                sb_ = cur[hp]
                                for hh in range(2):
                                    hd = hp * 2 + hh
                                    kb.mm(Obank[hd][:, c0:c0 + 128], sb_[hh * 64:(hh + 1) * 64, hh * 128:(hh + 1) * 128],
                                          q_[hh * 64:(hh + 1) * 64, hp, c0:c0 + 128], False, True, [sb_.r, q_.r], [Obank[hd].r])
                                pu = psr.next()
                                kb.mm(pu[:, 0:256], h_[:, ti, hp * 128:(hp + 1) * 128], v_[:, ti, hp * 256:(hp + 1) * 256],
                                      True, True, [h_.r, v_.r], [pu.r])
                                kb.stt(STf[hp][:], STf[hp][:], Aall[:, d, hp, n:n + 1], pu[:, 0:256], ALU.mult, ALU.add,
                                       [STf[hp].r, Aall.r, pu.r], [STf[hp].r])
                                nb_ = STb[hp].next()
                                kb.copy(nb_[:], STf[hp][:], [STf[hp].r], [nb_.r], eng="dve")
                                cur[hp] = nb_
                                pump2()
                                pump2()
                        for hd in range(4):
                            if d == 1:
                                of = f1.next()
                                kb.copy(of[:], Obank[hd][:], [Obank[hd].r], [of.r], eng="act")
                                kb.dma(obT_s[hd * 128:(hd + 1) * 128, t0:t0 + 512], of[:], reads=[of.r])
                        if d == 0:
                            g = c2_epilogue(t0, ol, sl)
                            next(g)
                            pend2.append(g)
                    while pend2:
                        pump2()
                while wj_state["next"] < len(wjobs) or wj_state["pend"]:
                    wj_step(wst2, wsb2, "act", n=2)
                P.barrier()

    def layer_norm_rows(h1, stats, mv, g_bc, b_bc, out_t):
        for c in range(2):
            P.op("dve", lambda e, c=c: e.bn_stats(out=stats[:, c * 6:(c + 1) * 6], in_=h1[:, c * 512:(c + 1) * 512]),
                 reads=[h1.r], writes=[stats.r])
        P.op("dve", lambda e: e.bn_aggr(out=mv[:, 0:2], in_=stats[:]), reads=[stats.r], writes=[mv.r])
        kb.act(mv[:, 2:3], mv[:, 1:2], AF.Sqrt, [mv.r], [mv.r], bias=LN_EPS)
        kb.recip(mv[:, 2:3], mv[:, 2:3], [mv.r], [mv.r])
        kb.ts(h1[:], h1[:], mv[:, 0:1], mv[:, 2:3], ALU.subtract, ALU.mult, [h1.r, mv.r], [h1.r])
        kb.tt(h1[:], h1[:], g_bc[:], ALU.mult, [h1.r, g_bc.r], [h1.r], eng="pool")
        kb.tt(out_t[:], h1[:], b_bc[:], ALU.add, [h1.r, b_bc.r], [out_t.r])

    if "D" in phases:
        with ExitStack() as st:
            wa = kb.sb(st, "wa", [128, 4, D], BF16)
            wb_ = kb.sb(st, "wb", [128, 4, D], BF16)
            wg = kb.sb(st, "wg", [128, 8, 2048], BF16)
            wo = kb.sb(st, "wo", [128, 8, D], BF16)
            bg = kb.sb(st, "bg", [128, 16], F32)
            g_bc = kb.sb(st, "g1bc", [128, D], F32)
            b_bc = kb.sb(st, "b1bc", [128, D], F32)
            ya = kb.sbring(st, "yaD", [128, 4, 512], BF16, 2)
            yb = kb.sbring(st, "ybD", [128, 4, 512], BF16, 2)
            xb = kb.sbring(st, "xbD", [128, 8, 512], BF16, 2)
            xr = kb.sbring(st, "xrD", [128, 4, D], F32, 3)
            mT = kb.sbring(st, "mT", [128, 8, 512], BF16, 2)
            f1 = kb.sbring(st, "f1D", [128, 512], F32, 6)
            stt_ = kb.sbring(st, "stD", [128, 12], F32, 2)
            mvr = kb.sbring(st, "mvD", [128, 4], F32, 2)
            psr = Ring(PS)
            def d_loads(tb):
                t0 = tb * 512
                ya_, yb_, xb_, xr_ = ya.next(), yb.next(), xb.next(), xr.next()
                kb.dma(ya_[:], yaT_s[:, t0:t0 + 512].rearrange("(h p) t -> p h t", p=128), writes=[ya_.r])
                kb.dma(yb_[:], ybT_s[:, t0:t0 + 512].rearrange("(h p) t -> p h t", p=128), writes=[yb_.r])
                kb.dma(xb_[:], xTb_s[:, t0:t0 + 512].rearrange("(c p) n -> p c n", p=128), writes=[xb_.r])
                kb.dma(xr_[:], x[t0:t0 + 512, :].rearrange("(s p) d -> p s d", p=128), writes=[xr_.r])
                return ya_, yb_, xb_, xr_

            LD = {0: d_loads(0)}
            kb.dma(wa[:], wab_s[:, :].rearrange("(c p) n -> p c n", p=128), writes=[wa.r])
            kb.dma(wb_[:], wbb_s[:, :].rearrange("(c p) n -> p c n", p=128), writes=[wb_.r])
            for c in range(2):
                kb.dma(wo[:, c * 4:(c + 1) * 4, :], wob_s[c * 512:(c + 1) * 512, :].rearrange("(c p) n -> p c n", p=128), writes=[wo.r])
            for c in range(4):
                kb.dma(wg[:, c * 2:(c + 1) * 2, :], wgb_s[c * 256:(c + 1) * 256, :].rearrange("(c p) n -> p c n", p=128), writes=[wg.r])
            kb.dma(bg[:], b_gate, writes=[bg.r])
            kb.dma(g_bc[:], ln1_g, writes=[g_bc.r])
            kb.dma(b_bc[:], ln1_b, writes=[b_bc.r])

            def d_merge(ld):
                ya_, yb_, xb_, xr_ = ld
                m_ = mT.next()
                for f in range(8):
                    pA, pB, pGa, pGb = psr.next(), psr.next(), psr.next(), psr.next()
                    for hc in range(4):
                        kb.mm(pA[:], wa[:, hc, f * 128:(f + 1) * 128], ya_[:, hc, :], hc == 0, hc == 3, [wa.r, ya_.r], [pA.r])
                    for hc in range(4):
                        kb.mm(pB[:], wb_[:, hc, f * 128:(f + 1) * 128], yb_[:, hc, :], hc == 0, hc == 3, [wb_.r, yb_.r], [pB.r])
                    for kc in range(8):
                        kb.mm(pGa[:], wg[:, kc, f * 128:(f + 1) * 128], xb_[:, kc, :], kc == 0, kc == 7, [wg.r, xb_.r], [pGa.r])
                    for kc in range(8):
                        kb.mm(pGb[:], wg[:, kc, 1024 + f * 128:1024 + (f + 1) * 128], xb_[:, kc, :], kc == 0, kc == 7, [wg.r, xb_.r], [pGb.r])
                    sa, sb2 = f1.next(), f1.next()
                    kb.act(sa[:], pGa[:], AF.Sigmoid, [pGa.r, bg.r], [sa.r], bias=bg[:, f:f + 1])
                    kb.act(sb2[:], pGb[:], AF.Sigmoid, [pGb.r, bg.r], [sb2.r], bias=bg[:, 8 + f:9 + f])
                    kb.tt(sa[:], sa[:], pA[:], ALU.mult, [sa.r, pA.r], [sa.r])
                    kb.tt(sb2[:], sb2[:], pB[:], ALU.mult, [sb2.r, pB.r], [sb2.r])
                    kb.tt(m_[:, f, :], sa[:], sb2[:], ALU.add, [sa.r, sb2.r], [m_.r], eng="pool")
                    if f % 2 == 1 and ln_jobs_d:
                        ln_jobs_d.pop(0)()
                return m_

            class RowView:
                def __init__(self, t, ts_):
                    self.t, self.ts_, self.r = t, ts_, t.r

                def __getitem__(self, idx):
                    if not isinstance(idx, tuple):
                        idx = (idx, slice(None))
                    return self.t[idx[0], self.ts_, idx[1]]

            def d_mix(tb, m_, xr_):
                t0 = tb * 512
                for ts_ in range(4):
                    for hf in range(2):
                        pm = psr.next()
                        for f in range(8):
                            kb.mm(pm[:], m_[:, f, ts_ * 128:(ts_ + 1) * 128], wo[:, f, hf * 512:(hf + 1) * 512], f == 0, f == 7,
                                  [m_.r, wo.r], [pm.r])
                        kb.stt(xr_[:, ts_, hf * 512:(hf + 1) * 512], xr_[:, ts_, hf * 512:(hf + 1) * 512], ALPHA, pm[:], ALU.mult, ALU.add,
                               [xr_.r, pm.r], [xr_.r])
                for ts_ in range(4):
                    def ln_job(v=RowView(xr_, ts_), r_=t0 + ts_ * 128):
                        layer_norm_rows(v, stt_.next(), mvr.next(), g_bc, b_bc, v)
                        kb.dma(x1_s[r_:r_ + 128, :], v[:], reads=[v.r])
                    ln_jobs_d.append(ln_job)

            ln_jobs_d = []
            LD[1] = d_loads(1)
            m_cur = d_merge(LD[0])
            for tb in range(NBLK):
                if tb + 1 < NBLK:
                    m_nxt = d_merge(LD[tb + 1])
                if tb + 2 < NBLK:
                    LD[tb + 2] = d_loads(tb + 2)
                d_mix(tb, m_cur, LD.pop(tb)[3])
                if tb + 1 < NBLK:
                    m_cur = m_nxt
            while ln_jobs_d:
                ln_jobs_d.pop(0)()
            P.barrier()

    if "E" in phases:
        with ExitStack() as st:
            w1c = [kb.sb(st, f"w1_{i}", [128, 8, 512], BF16) for i in range(8)]
            g_bc = kb.sb(st, "g2bc", [128, D], F32)
            b_bc = kb.sb(st, "b2bc", [128, D], F32)
            x1l = kb.sbring(st, "x1E", [128, D], F32, 12)
            x1T = kb.sbring(st, "x1T", [128, 8, 512], BF16, 2)
            hT = kb.sb(st, "hT", [128, 32, 512], BF16)
            w2r = kb.sbring(st, "w2E", [128, 4, D], BF16, 2)
            rl = kb.sbring(st, "rlE", [128, 512], F32, 2)
            stt_ = kb.sbring(st, "stE", [128, 12], F32, 2)
            mvr = kb.sbring(st, "mvE", [128, 4], F32, 2)
            psr = Ring(PS)

            def e_loads(tb):
                xs_ = []
                for ts_ in range(4):
                    xl = x1l.next()
                    r0 = tb * 512 + ts_ * 128
                    kb.dma(xl[:], x1_s[r0:r0 + 128, :], writes=[xl.r])
                    xs_.append(xl)
                return xs_

            def w2_load(c):
                w_ = w2r.next()
                kb.dma(w_[:], w2b_s[c * 512:(c + 1) * 512, :].rearrange("(c p) n -> p c n", p=128), writes=[w_.r])
                return w_

            def e_transposes(xls):
                xt = x1T.next()
                for ts_ in range(4):
                    for g4 in range(2):
                        ptp = psr.next()
                        for j in range(4):
                            fc = g4 * 4 + j
                            kb.tr(ptp[:, j * 128:(j + 1) * 128], xls[ts_][:, fc * 128:(fc + 1) * 128], ident, [xls[ts_].r, cm.r], [ptp.r])
                        kb.copy(xt[:, g4 * 4:(g4 + 1) * 4, ts_ * 128:(ts_ + 1) * 128], ptp[:].rearrange("p (j t) -> p j t", j=4),
                                [ptp.r], [xt.r], eng="act")
                return xt

            ln_jobs = []
            xls = e_loads(0)
            for c in range(8):
                kb.dma(w1c[c][:], w1b_s[:, c * 512:(c + 1) * 512].rearrange("(c p) n -> p c n", p=128), writes=[w1c[c].r])
            kb.dma(g_bc[:], ln2_g, writes=[g_bc.r])
            kb.dma(b_bc[:], ln2_b, writes=[b_bc.r])
            xt = e_transposes(xls)
            for tb in range(NBLK):
                t0 = tb * 512
                if tb + 1 < NBLK:
                    xls_n = e_loads(tb + 1)
                w2q = [w2_load(0), w2_load(1)]
                for ff in range(32):
                    ph = psr.next()
                    for kc in range(8):
                        kb.mm(ph[:], w1c[ff // 4][:, kc, (ff % 4) * 128:(ff % 4 + 1) * 128], xt[:, kc, :], kc == 0, kc == 7,
                              [w1c[ff // 4].r, xt.r], [ph.r])
                    r_ = rl.next()
                    kb.act(r_[:], ph[:], AF.Relu, [ph.r], [r_.r])
                    kb.tt(hT[:, ff, :], r_[:], ph[:], ALU.mult, [r_.r, ph.r], [hT.r])
                    if ff % 6 == 5 and ln_jobs:
                        ln_jobs.pop(0)()
                if tb + 1 < NBLK:
                    xt_n = e_transposes(xls_n)
                for c in range(8):
                    w_ = w2q.pop(0)
                    for ts_ in range(4):
                        for hf in range(2):
                            pm = PS[ts_ * 2 + hf]
                            for j in range(4):
                                kb.mm(pm[:], hT[:, c * 4 + j, ts_ * 128:(ts_ + 1) * 128], w_[:, j, hf * 512:(hf + 1) * 512],
                                      c == 0 and j == 0, c == 7 and j == 3, [hT.r, w_.r], [pm.r])
                    if c + 2 < 8:
                        w2q.append(w2_load(c + 2))
                for ts_ in range(4):
                    for hf in range(2):
                        pm = PS[ts_ * 2 + hf]
                        kb.stt(xls[ts_][:, hf * 512:(hf + 1) * 512], xls[ts_][:, hf * 512:(hf + 1) * 512], ALPHA, pm[:], ALU.mult, ALU.add,
                               [xls[ts_].r, pm.r], [xls[ts_].r])
                for ts_ in range(4):
                    def ln_job(xl=xls[ts_], r0=t0 + ts_ * 128):
                        layer_norm_rows(xl, stt_.next(), mvr.next(), g_bc, b_bc, xl)
                        kb.dma(y[r0:r0 + 128, :], xl[:], reads=[xl.r], is_out=True)
                    ln_jobs.append(ln_job)
                if tb + 1 < NBLK:
                    xls, xt = xls_n, xt_n
            while ln_jobs:
                ln_jobs.pop(0)()
    else:
        dummy = kb.sb(gst, "dummy_y", [128, D], F32)
        kb.memset(dummy[:], 0.0, [dummy.r])
        kb.dma(y[0:128, :], dummy[:], reads=[dummy.r], is_out=True)

    P.emit()
    gst.close()
    return nc


def _consts():
    pos = np.arange(S, dtype=np.float32)
    inv = (1.0 / (np.float32(500000.0) ** (np.arange(0, 16, 2, dtype=np.float32) / np.float32(16)))).astype(np.float32)
    ang = (pos[:, None] * inv[None, :]).astype(np.float32)
    cos, sin = np.cos(ang).astype(np.float32), np.sin(ang).astype(np.float32)
    C = np.ones((128, S), np.float32)
    Sg = np.zeros((128, S), np.float32)
    for p in range(128):
        d = p % 64
        if d < 8:
            C[p] = cos[:, d]
            Sg[p] = -sin[:, d]
        elif d < 16:
            C[p] = cos[:, d - 8]
            Sg[p] = sin[:, d - 8]
    i = np.arange(128)
    same = np.ones((128, 128), bool)
    cm = np.zeros((128, 6, 128), np.float32)
    cm[:, 0] = same & (i[:, None] <= i[None, :])
    cm[:, 1] = same & (i[:, None] >= i[None, :])
    cm[:, 2] = same & (i[:, None] > i[None, :])
    cm[:, 3] = same & (i[:, None] < i[None, :])
    cm[:, 4] = np.eye(128, dtype=np.float32)
    mk = np.zeros((128, 2, 256), np.float32)
    mf = (same & (i[:, None] <= i[None, :])).astype(np.float32)
    mb = (same & (i[:, None] >= i[None, :])).astype(np.float32)
    mk[:, 0, 0:128] = mf
    mk[:, 0, 128:256] = mf
    mk[:, 1, 0:128] = mb
    mk[:, 1, 128:256] = mb
    return C, Sg, cm, mk


def _rot_perm():
    perm = np.arange(1024)
    for c in range(1024):
        d = c % 64
        if d < 8:
            perm[c] = c + 8
        elif d < 16:
            perm[c] = c - 8
    return perm


def make_in_maps(inputs):
    f = lambda a: np.ascontiguousarray(np.asarray(a, dtype=np.float32))
    C, Sg, cm, mk = _consts()
    w_in = f(inputs["w_in"][0])
    perm = _rot_perm()
    w_rot = np.ascontiguousarray(w_in[:, :1024][:, perm])
    lamv = np.concatenate([f(inputs["lam_q1"][0]), f(inputs["lam_k1"][0]), f(inputs["lam_q2"][0]), f(inputs["lam_k2"][0])])[None, :]
    shared = {
        "w_in": w_in, "w_rot": w_rot, "ropeC": C, "ropeS": Sg, "lamv": f(lamv),
        "subln_g": f(inputs["subln_g"][0]).reshape(128, 1), "gnorm_g": f(inputs["gla_norm_g"][0]).reshape(128, 1),
        "w_dec_f": f(inputs["w_dec_f"][0]), "w_dec_b": f(inputs["w_dec_b"][0]),
        "b_dec_f": f(inputs["b_dec_f"][0]).reshape(1, 256), "b_dec_b": f(inputs["b_dec_b"][0]).reshape(1, 256),
        "w_br_a": f(inputs["w_br_a"][0]), "w_br_b": f(inputs["w_br_b"][0]),
        "b_gate": np.ascontiguousarray(f(inputs["b_gate"][0]).reshape(16, 128).T),
        "w_out": f(inputs["w_out"][0]),
        "ln1_g": np.ascontiguousarray(np.broadcast_to(f(inputs["ln1_g"][0]).reshape(1, D), (128, D))), "ln1_b": np.ascontiguousarray(np.broadcast_to(f(inputs["ln1_b"][0]).reshape(1, D), (128, D))),
        "ln2_g": np.ascontiguousarray(np.broadcast_to(f(inputs["ln2_g"][0]).reshape(1, D), (128, D))), "ln2_b": np.ascontiguousarray(np.broadcast_to(f(inputs["ln2_b"][0]).reshape(1, D), (128, D))),
        "w_mlp1": f(inputs["w_mlp1"][0]), "w_mlp2": f(inputs["w_mlp2"][0]),
        "cmats": cm, "cmask": mk,
    }
    seqs = [f(inputs["x_prompt"][i]) for i in range(4)] + [f(inputs["x_sample"][i]) for i in range(2)]
    seqs = seqs + [seqs[0], seqs[1]]
    maps = []
    for c in range(8):
        m = dict(shared)
        m["x"] = seqs[c]
        m["xT"] = np.ascontiguousarray(seqs[c].T)
        maps.append(m)
    return maps


_NC = None


def kernel(**inputs):
    global _NC
    if _NC is None:
        _NC = build()
    maps = make_in_maps(inputs)
    res = run_bass_kernel_spmd(_NC, maps, core_ids=list(range(8)))
    outs = [np.asarray(res.results[c]["y"], dtype=np.float32) for c in range(6)]
    y_prompt = np.stack(outs[0:4], axis=0)
    y_sample = np.stack(outs[4:6], axis=0)
    return (y_prompt, y_sample)
```

```python
import math
from contextlib import ExitStack

import numpy as np
import concourse.bass as bass
import concourse.mybir as mybir
from concourse.bass_utils import run_bass_kernel_spmd

F32 = mybir.dt.float32
BF16 = mybir.dt.bfloat16
AF = mybir.ActivationFunctionType
ALU = mybir.AluOpType

S = 8192
D = 1024
NBLK = 16
NHEADS_B = 4
CORE_OF_SEQ = [0, 1, 2, 4, 5, 6]
ALPHA = 2.0 ** 0.25
LAM_INIT = 0.8 - 0.6 * math.exp(0.0)
LN_EPS = 1e-5
RMS_EPS = 1e-6


class Res:
    __slots__ = ("name", "lw", "lr", "dr", "track")

    def __init__(self, name):
        self.name = name
        self.track = True
        self.lw = []
        self.lr = {}
        self.dr = []


class Op:
    __slots__ = ("eng", "fn", "deps", "dma", "tok", "sig", "sigval")

    def __init__(self, eng, fn, deps, dma, tok):
        self.eng = eng
        self.fn = fn
        self.deps = deps
        self.dma = dma
        self.tok = tok
        self.sig = False
        self.sigval = 0


class Prog:
    ENGS = ("pe", "act", "dve", "pool", "sp")
    ND = 48

    def __init__(self, nc):
        self.nc = nc
        self.ops = {e: [] for e in self.ENGS}
        self.dmas = []
        self.out_toks = []
        self.bar_dma = 0

    def op(self, eng, fn, reads=(), writes=(), dma=False, out=False):
        reads = [r for r in reads if r.track]
        writes = [w for w in writes if w.track]
        deps = set()
        for r in reads:
            deps.update(r.lw)
        wdeps = set()
        join = {}
        for w in writes:
            j = dma and bool(w.lw) and all(t[0] == "d" for t in w.lw) and not w.lr and not w.dr
            join[id(w)] = j
            if not j:
                wdeps.update(w.lw)
            wdeps.update(w.lr.values())
            wdeps.update(w.dr)
        deps |= wdeps
        if dma:
            k = len(self.dmas)
            tok = ("d", k)
            if k >= self.ND:
                deps.add(("d", k - self.ND))
        else:
            tok = ("e", eng, len(self.ops[eng]))
        if eng == "pe" and not dma:
            deps = {d for d in deps if not (d[0] == "e" and d[1] == "pe")}
        o = Op(eng, fn, deps, dma, tok)
        self.ops[eng].append(o)
        if dma:
            self.dmas.append(o)
            if out:
                self.out_toks.append(tok)
        for r in reads:
            if dma:
                r.dr.append(tok)
            else:
                r.lr[eng] = tok
        for w in writes:
            if join[id(w)]:
                w.lw.append(tok)
            else:
                w.lw = [tok]
            w.lr = {}
            w.dr = []
        return tok

    def barrier(self):
        last = {}
        for e in self.ENGS:
            for o in reversed(self.ops[e]):
                if not o.dma and o.fn is not None:
                    last[e] = o.tok
                    break
        dtoks = {("d", k) for k in range(max(self.bar_dma, len(self.dmas) - self.ND), len(self.dmas))}
        self.bar_dma = len(self.dmas)
        for e in self.ENGS:
            deps = set(dtoks) | {t for ee, t in last.items() if ee != e}
            o = Op(e, None, deps, False, ("e", e, len(self.ops[e])))
            self.ops[e].append(o)

    def emit(self):
        nc = self.nc
        fin = Op("sp", None, set(self.out_toks), False, ("e", "sp", len(self.ops["sp"])))
        self.ops["sp"].append(fin)
        for e in self.ENGS:
            for o in self.ops[e]:
                for d in o.deps:
                    if d[0] == "e":
                        self.ops[d[1]][d[2]].sig = True
        for e in self.ENGS:
            c = 0
            for o in self.ops[e]:
                if o.sig and not o.dma and o.fn is not None:
                    c += 1
                o.sigval = c
        esem = {e: nc.alloc_semaphore(name=f"tl_{e}") for e in self.ENGS}
        dsem = [nc.alloc_semaphore(name=f"dq_{i}") for i in range(self.ND)]
        engobj = {"pe": "tensor", "act": "scalar", "dve": "vector", "pool": "gpsimd", "sp": "sync"}
        with nc.Block() as block:
            for e in self.ENGS:
                ops = self.ops[e]

                def body(eng, ops=ops):
                    waited = {}
                    for o in ops:
                        need = {}
                        for d in o.deps:
                            if d[0] == "d":
                                k = d[1]
                                s = dsem[k % self.ND]
                                v = 16 * (k // self.ND + 1)
                            else:
                                s = esem[d[1]]
                                v = self.ops[d[1]][d[2]].sigval
                            if v > need.get(s.num, (None, 0))[1]:
                                need[s.num] = (s, v)
                        for sn, (s, v) in need.items():
                            if waited.get(sn, 0) < v:
                                eng.wait_ge(s, v)
                                waited[sn] = v
                        if o.fn is None:
                            continue
                        ins = o.fn(eng)
                        if o.dma:
                            k = o.tok[1]
                            ins.then_inc(dsem[k % self.ND], 16)
                        elif o.sig:
                            ins.then_inc(esem[o.eng], 1)

                getattr(block, engobj[e])(body)


class T:
    __slots__ = ("t", "r")

    def __init__(self, t, name):
        self.t = t
        self.r = Res(name)

    def __getitem__(self, idx):
        return self.t[idx]


class Ring:
    def __init__(self, items):
        self.items = items
        self.i = 0

    def next(self):
        x = self.items[self.i % len(self.items)]
        self.i += 1
        return x


class KB:
    def __init__(self, debug=()):
        self.nc = bass.Bass("TRN2", target_bir_lowering=False)
        self.P = Prog(self.nc)
        self.debug = set(debug)

    def dram_in(self, name, shape, dt=F32):
        return self.nc.dram_tensor(name, list(shape), dt, kind="ExternalInput").ap()

    def dram_out(self, name, shape, dt=F32):
        return self.nc.dram_tensor(name, list(shape), dt, kind="ExternalOutput").ap()

    def dram_tmp(self, name, shape, dt=F32):
        kind = "ExternalOutput" if name in self.debug else "Internal"
        t = T(self.nc.dram_tensor(name, list(shape), dt, kind=kind).ap(), name)
        t.r.track = False
        return t

    def sb(self, st, name, shape, dt=F32):
        return T(st.enter_context(self.nc.sbuf_tensor(name, list(shape), dt)), name)

    def sbring(self, st, name, shape, dt, n):
        return Ring([self.sb(st, f"{name}{i}", shape, dt) for i in range(n)])

    def mm(self, out, lhsT, rhs, start, stop, reads, writes):
        self.P.op("pe", lambda e: e.matmul(out, lhsT=lhsT, rhs=rhs, start=start, stop=stop, skip_group_check=True),
                  reads=reads, writes=writes)

    def tr(self, out, in_, ident, reads, writes):
        self.P.op("pe", lambda e: e.transpose(out, in_, ident), reads=reads, writes=writes)

    def act(self, out, in_, func, reads, writes, scale=1.0, bias=None, eng="act"):
        if bias is None:
            self.P.op(eng, lambda e: e.activation(out=out, in_=in_, func=func, scale=scale), reads=reads, writes=writes)
        else:
            self.P.op(eng, lambda e: e.activation(out=out, in_=in_, func=func, scale=scale, bias=bias), reads=reads, writes=writes)

    def tt(self, out, in0, in1, op, reads, writes, eng="dve"):
        self.P.op(eng, lambda e: e.tensor_tensor(out=out, in0=in0, in1=in1, op=op), reads=reads, writes=writes)

    def ts(self, out, in0, s1, s2, op0, op1, reads, writes, eng="dve"):
        if op1 is None:
            self.P.op(eng, lambda e: e.tensor_scalar(out=out, in0=in0, scalar1=s1, scalar2=None, op0=op0), reads=reads, writes=writes)
        else:
            self.P.op(eng, lambda e: e.tensor_scalar(out=out, in0=in0, scalar1=s1, scalar2=s2, op0=op0, op1=op1), reads=reads, writes=writes)

    def stt(self, out, in0, scalar, in1, op0, op1, reads, writes):
        self.P.op("dve", lambda e: e.scalar_tensor_tensor(out=out, in0=in0, scalar=scalar, in1=in1, op0=op0, op1=op1),
                  reads=reads, writes=writes)

    def copy(self, out, in_, reads, writes, eng="dve"):
        if eng == "act":
            self.P.op("act", lambda e: e.copy(out=out, in_=in_), reads=reads, writes=writes)
        else:
            self.P.op(eng, lambda e: e.tensor_copy(out=out, in_=in_), reads=reads, writes=writes)

    def recip(self, out, in_, reads, writes):
        self.P.op("dve", lambda e: e.reciprocal(out=out, in_=in_), reads=reads, writes=writes)

    def memset(self, ap, val, writes, eng="dve"):
        self.P.op(eng, lambda e: e.memset(ap, val), writes=writes)

    def dma(self, out, in_, reads=(), writes=(), eng="sp", is_out=False):
        self.P.op(eng, lambda e: e.dma_start(out=out, in_=in_), reads=reads, writes=writes, dma=True, out=is_out)


def build(phases="ABCGDE", debug=()):
    kb = KB(debug)
    nc, P = kb.nc, kb.P
    xT = kb.dram_in("xT", [D, S])
    x = kb.dram_in("x", [S, D])
    w_in = kb.dram_in("w_in", [D, 5152])
    w_rot = kb.dram_in("w_rot", [D, 1024])
    ropeC = kb.dram_in("ropeC", [128, S])
    ropeS = kb.dram_in("ropeS", [128, S])
    lamv = kb.dram_in("lamv", [1, 256])
    subln_g = kb.dram_in("subln_g", [128, 1])
    gnorm_g = kb.dram_in("gnorm_g", [128, 1])
    w_dec_f = kb.dram_in("w_dec_f", [16, 256])
    w_dec_b = kb.dram_in("w_dec_b", [16, 256])
    b_dec_f = kb.dram_in("b_dec_f", [1, 256])
    b_dec_b = kb.dram_in("b_dec_b", [1, 256])
    w_br_a = kb.dram_in("w_br_a", [512, D])
    w_br_b = kb.dram_in("w_br_b", [512, D])
    b_gate = kb.dram_in("b_gate", [128, 16])
    w_out = kb.dram_in("w_out", [D, D])
    ln1_g = kb.dram_in("ln1_g", [128, D])
    ln1_b = kb.dram_in("ln1_b", [128, D])
    ln2_g = kb.dram_in("ln2_g", [128, D])
    ln2_b = kb.dram_in("ln2_b", [128, D])
    w_mlp1 = kb.dram_in("w_mlp1", [D, 4096])
    w_mlp2 = kb.dram_in("w_mlp2", [4096, D])
    cmats = kb.dram_in("cmats", [128, 6, 128])
    cmask = kb.dram_in("cmask", [128, 2, 256])
    y = kb.dram_out("y", [S, D])

    qT_s = kb.dram_tmp("qT_s", [512, S], BF16)
    kT_s = kb.dram_tmp("kT_s", [512, S], BF16)
    v_s = kb.dram_tmp("v_s", [4, 128, 64, 128], BF16)
    gqT_s = kb.dram_tmp("gqT_s", [256, S])
    gkT_s = kb.dram_tmp("gkT_s", [256, S])
    srT_s = kb.dram_tmp("srT_s", [512, S])
    zT_s = kb.dram_tmp("zT_s", [32, S])
    gk_s = kb.dram_tmp("gk_s", [S, 256])
    gv_s = kb.dram_tmp("gv_s", [S, 512], BF16)
    yaT_s = kb.dram_tmp("yaT_s", [512, S], BF16)
    ybT_s = kb.dram_tmp("ybT_s", [512, S], BF16)
    qtT_s = [kb.dram_tmp(f"qtT_s{d}", [256, S], BF16) for d in range(2)]
    ktT_s = [kb.dram_tmp(f"ktT_s{d}", [256, S], BF16) for d in range(2)]
    kh_s = [kb.dram_tmp(f"kh_s{d}", [S, 256], BF16) for d in range(2)]
    obT_s = kb.dram_tmp("obT_s", [512, S])
    x1_s = kb.dram_tmp("x1_s", [S, D])
    xTb_s = kb.dram_tmp("xTb_s", [D, S], BF16)
    w1b_s = kb.dram_tmp("w1b_s", [D, 4096], BF16)
    w2b_s = kb.dram_tmp("w2b_s", [4096, D], BF16)
    wgb_s = kb.dram_tmp("wgb_s", [D, 2048], BF16)
    wab_s = kb.dram_tmp("wab_s", [512, D], BF16)
    wbb_s = kb.dram_tmp("wbb_s", [512, D], BF16)
    wob_s = kb.dram_tmp("wob_s", [D, D], BF16)

    psall = nc.alloc_psum_tensor("psall", [128, 4096], F32)

    class Bank:
        def __init__(self, i):
            self.i = i
            self.r = Res(f"ps{i}")

        def __getitem__(self, idx):
            if not isinstance(idx, tuple):
                idx = (idx, slice(None))
            p, f = idx
            a = 0 if f.start is None else f.start
            b = 512 if f.stop is None else f.stop
            return psall[p, self.i * 512 + a:self.i * 512 + b]

    PS = [Bank(i) for i in range(8)]
    gst = ExitStack()
    cm = kb.sb(gst, "cm", [128, 6, 128], F32)
    msk = kb.sb(gst, "msk", [128, 2, 256], F32)
    ones_r = kb.sb(gst, "ones_r", [1, 128], F32)
    ones_c = kb.sb(gst, "ones_c", [128, 1], F32)
    ones_cb = kb.sb(gst, "ones_cb", [128, 1], BF16)
    lam_t = kb.sb(gst, "lam_t", [1, 8], F32)
    lam_w = kb.sb(gst, "lam_w", [1, 256], F32)
    sg_a = kb.sb(gst, "sg_a", [128, 1], F32)
    sg_b = kb.sb(gst, "sg_b", [128, 1], F32)
    Aall = kb.sb(gst, "Aall", [128, 2, 2, 64], F32)

    kb.dma(cm[:], cmats, writes=[cm.r])
    kb.dma(msk[:], cmask, writes=[msk.r])
    kb.dma(lam_w[:], lamv, writes=[lam_w.r])
    kb.dma(sg_a[:], subln_g, writes=[sg_a.r])
    kb.dma(sg_b[:], gnorm_g, writes=[sg_b.r])
    kb.memset(ones_r[:], 1.0, [ones_r.r])
    kb.memset(ones_c[:], 1.0, [ones_c.r])
    kb.memset(ones_cb[:], 1.0, [ones_cb.r])
    kb.tt(lam_w[:, 0:64], lam_w[:, 0:64], lam_w[:, 64:128], ALU.mult, [lam_w.r], [lam_w.r])
    kb.tt(lam_w[:, 128:192], lam_w[:, 128:192], lam_w[:, 192:256], ALU.mult, [lam_w.r], [lam_w.r])
    P.op("dve", lambda e: e.reduce_sum(out=lam_t[:, 0:1], in_=lam_w[:, 0:64], axis=mybir.AxisListType.X), reads=[lam_w.r], writes=[lam_t.r])
    P.op("dve", lambda e: e.reduce_sum(out=lam_t[:, 1:2], in_=lam_w[:, 128:192], axis=mybir.AxisListType.X), reads=[lam_w.r, lam_t.r], writes=[lam_t.r])
    kb.act(lam_t[:, 0:2], lam_t[:, 0:2], AF.Exp, [lam_t.r], [lam_t.r])
    kb.tt(lam_t[:, 2:3], lam_t[:, 1:2], lam_t[:, 0:1], ALU.subtract, [lam_t.r], [lam_t.r])
    kb.ts(lam_t[:, 2:3], lam_t[:, 2:3], -LAM_INIT, None, ALU.add, None, [lam_t.r], [lam_t.r])
    kb.ts(sg_a[:], sg_a[:], 1.0 - LAM_INIT, None, ALU.mult, None, [sg_a.r], [sg_a.r])
    if "lam" in debug:
        lam_o = kb.dram_out("lam_o", [1, 8])
        kb.dma(lam_o, lam_t[:], reads=[lam_t.r], is_out=True)

    ident = cm[:, 4, :]

    def rms_bcast(Dt, sq, row, psA, psB):
        kb.act(sq[:], Dt[:], AF.Square, [Dt.r], [sq.r])
        kb.mm(psA[0:1, :], ones_c[:], sq[:], True, True, [ones_c.r, sq.r], [psA.r])
        kb.act(row[:], psA[0:1, :], AF.Sqrt, [psA.r], [row.r], scale=1.0 / 128.0, bias=RMS_EPS)
        kb.recip(row[:], row[:], [row.r], [row.r])
        kb.mm(psB[:], ones_r[:], row[:], True, True, [ones_r.r, row.r], [psB.r])

    wjobs = []
    for c in range(8):
        for hf in range(2):
            wjobs.append((w_mlp1[c * 128:(c + 1) * 128, hf * 2048:(hf + 1) * 2048], w1b_s[c * 128:(c + 1) * 128, hf * 2048:(hf + 1) * 2048], None))
    for c in range(16):
        wjobs.append((w_mlp2[c * 256:(c + 1) * 256, :].rearrange("(a p) n -> p a n", p=128),
                      w2b_s[c * 256:(c + 1) * 256, :].rearrange("(a p) n -> p a n", p=128), 2))
    for c in range(8):
        wjobs.append((w_in[c * 128:(c + 1) * 128, 3104:5152], wgb_s[c * 128:(c + 1) * 128, :], None))
    for src_, dst_, nr in ((w_br_a, wab_s, 2), (w_br_b, wbb_s, 2), (w_out, wob_s, 4)):
        for c in range(nr):
            wjobs.append((src_[c * 256:(c + 1) * 256, :].rearrange("(a p) n -> p a n", p=128),
                          dst_[c * 256:(c + 1) * 256, :].rearrange("(a p) n -> p a n", p=128), 2))
    wj_state = {"next": 0, "pend": []}

    def wj_step(wst, wsb, eng, n=1):
        for (j, s_) in wj_state["pend"]:
            src_, dst_, a = wjobs[j]
            b_ = wsb.next()
            bv = b_[:] if a is None else b_[:].rearrange("p (a n) -> p a n", a=a)
            kb.copy(b_[:], s_[:], [s_.r], [b_.r], eng=eng)
            kb.dma(dst_, bv, reads=[b_.r])
        wj_state["pend"] = []
        for _ in range(n):
            j = wj_state["next"]
            if j >= len(wjobs):
                return
            wj_state["next"] = j + 1
            src_, dst_, a = wjobs[j]
            s_ = wst.next()
            sv = s_[:] if a is None else s_[:].rearrange("p (a n) -> p a n", a=a)
            kb.dma(sv, src_, writes=[s_.r])
            wj_state["pend"].append((j, s_))

    if "A" in phases:
        with ExitStack() as st:
            wFc = [kb.sb(st, f"wF{i}", [128, 8, 512], BF16) for i in range(7)]
            wTc = [kb.sb(st, f"wT{i}", [128, 8, 512], BF16) for i in range(3)]
            xb = kb.sbring(st, "xb", [128, 8, 512], BF16, 2)
            rc = kb.sbring(st, "rc", [128, 512], F32, 2)
            rs = kb.sbring(st, "rs", [128, 512], F32, 2)
            tmp = kb.sbring(st, "tmpA", [128, 512], F32, 4)
            stb = kb.sbring(st, "stbA", [128, 512], BF16, 4)
            stf = kb.sbring(st, "stfA", [128, 512], F32, 4)
            psr = Ring(PS)

            xs = kb.sbring(st, "xsA", [128, 8, 512], F32, 2)
            cvt_i = [0]

            def wload(dst, src, s0, n):
                s_ = xs.next()
                kb.dma(s_[:, :, 0:n], src[:, s0:s0 + n].rearrange("(c p) n -> p c n", p=128), writes=[s_.r])
                kb.copy(dst[:, :, 0:n], s_[:, :, 0:n], [s_.r], [dst.r], eng=("dve", "act", "pool")[cvt_i[0] % 3])
                cvt_i[0] += 1

            wload(wFc[0], w_in, 0, 512)
            wload(wFc[1], w_rot, 0, 512)
            wload(wFc[2], w_in, 512, 512)
            wload(wFc[3], w_rot, 512, 512)
            wload(wFc[4], w_in, 1536, 512)
            wload(wFc[5], w_in, 2560, 512)
            wload(wFc[6], w_in, 3072, 32)
            wload(wTc[0], w_in, 1024, 512)
            wload(wTc[1], w_in, 2048, 512)
            wload(wTc[2], w_in, 1792, 256)

            def fm(xt, j, ps, m=128):
                w_ = wFc[j // 4]
                o = (j % 4) * 128
                for kc in range(8):
                    kb.mm(ps[0:m, :], w_[:, kc, o:o + m], xt[:, kc, :], kc == 0, kc == 7, [w_.r, xt.r], [ps.r])

            def a_dma(tb):
                t0 = tb * 512
                xt, xs_ = xb.next(), xs.next()
                kb.dma(xs_[:], xT[:, t0:t0 + 512].rearrange("(c p) n -> p c n", p=128), writes=[xs_.r])
                c_t, s_t = rc.next(), rs.next()
                kb.dma(c_t[:], ropeC[:, t0:t0 + 512], writes=[c_t.r])
                kb.dma(s_t[:], ropeS[:, t0:t0 + 512], writes=[s_t.r])
                return xt, c_t, s_t, xs_, t0

            def a_cvt(ld):
                xt, c_t, s_t, xs_, t0 = ld
                kb.copy(xt[:, 0:4, :], xs_[:, 0:4, :], [xs_.r], [xt.r], eng="dve")
                kb.copy(xt[:, 4:8, :], xs_[:, 4:8, :], [xs_.r], [xt.r], eng="pool")
                kb.dma(xTb_s[:, t0:t0 + 512].rearrange("(c p) n -> p c n", p=128), xt[:], reads=[xt.r])

            nxtA = a_dma(0)
            a_cvt(nxtA)
            for tb in range(NBLK):
                t0 = tb * 512
                xt, c_t, s_t = nxtA[0:3]
                if tb + 1 < NBLK:
                    nxtA = a_dma(tb + 1)
                for base, dst in ((0, qT_s), (8, kT_s)):
                    for h in range(4):
                        p1, p2 = psr.next(), psr.next()
                        fm(xt, base + h, p1)
                        fm(xt, base + 4 + h, p2)
                        a1, a2, ob = tmp.next(), tmp.next(), stb.next()
                        kb.tt(a1[:], p1[:], c_t[:], ALU.mult, [p1.r, c_t.r], [a1.r])
                        kb.tt(a2[:], p2[:], s_t[:], ALU.mult, [p2.r, s_t.r], [a2.r])
                        kb.tt(ob[:], a1[:], a2[:], ALU.add, [a1.r, a2.r], [ob.r])
                        kb.dma(dst[h * 128:(h + 1) * 128, t0:t0 + 512], ob[:], reads=[ob.r], writes=[dst.r])
                for j, dst in ((16, gqT_s), (17, gqT_s), (18, gkT_s), (19, gkT_s)):
                    p1 = psr.next()
                    fm(xt, j, p1)
                    of = stf.next()
                    kb.copy(of[:], p1[:], [p1.r], [of.r], eng="act")
                    r0 = (j % 2) * 128
                    kb.dma(dst[r0:r0 + 128, t0:t0 + 512], of[:], reads=[of.r], writes=[dst.r])
                if tb + 1 < NBLK:
                    a_cvt(nxtA)
                for h in range(4):
                    p1 = psr.next()
                    fm(xt, 20 + h, p1)
                    of = stf.next()
                    kb.act(of[:], p1[:], AF.Silu, [p1.r], [of.r])
                    kb.dma(srT_s[h * 128:(h + 1) * 128, t0:t0 + 512], of[:], reads=[of.r], writes=[srT_s.r])
                p1 = psr.next()
                fm(xt, 24, p1, m=32)
                of = stf.next()
                kb.copy(of[0:32, :], p1[0:32, :], [p1.r], [of.r], eng="act")
                kb.dma(zT_s[:, t0:t0 + 512], of[0:32, :], reads=[of.r], writes=[zT_s.r])
                for ts_ in range(4):
                    n = tb * 4 + ts_
                    for grp in range(3):
                        ncol = 256 if grp == 2 else 512
                        p1 = psr.next()
                        for kc in range(8):
                            kb.mm(p1[:, 0:ncol], xt[:, kc, ts_ * 128:(ts_ + 1) * 128], wTc[grp][:, kc, 0:ncol],
                                  kc == 0, kc == 7, [wTc[grp].r, xt.r], [p1.r])
                        if grp == 0:
                            ob = stb.next()
                            kb.copy(ob[:], p1[:], [p1.r], [ob.r], eng="act")
                            kb.dma(v_s[:, :, n, :].rearrange("h p d -> p h d"), ob[:].rearrange("p (h d) -> p h d", h=4),
                                   reads=[ob.r], writes=[v_s.r])
                        elif grp == 1:
                            ob = stb.next()
                            kb.copy(ob[:], p1[:], [p1.r], [ob.r], eng="dve")
                            kb.dma(gv_s[n * 128:(n + 1) * 128, :], ob[:], reads=[ob.r], writes=[gv_s.r])
                        else:
                            of = stf.next()
                            kb.copy(of[:, 0:256], p1[:, 0:256], [p1.r], [of.r], eng="act")
                            kb.dma(gk_s[n * 128:(n + 1) * 128, :], of[:, 0:256], reads=[of.r], writes=[gk_s.r])
            P.barrier()

    if "B" in phases:
        with ExitStack() as st:
            KT = kb.sbring(st, "KT", [128, S], BF16, 2)
            Q0 = kb.sbring(st, "Q0p", [128, S], BF16, 2)
            Q1 = kb.sbring(st, "Q1p", [128, S], BF16, 2)
            VV = kb.sbring(st, "VV", [128, 64, 128], BF16, 2)
            pt = kb.sbring(st, "ptB", [128, 2, 512], BF16, 8)
            accD = [kb.sbring(st, f"accD{p}", [128, 512], F32, 2) for p in range(2)]
            accD1 = kb.sbring(st, "accM1_", [128, 512], F32, 2)
            f1 = kb.sbring(st, "f1B", [128, 512], F32, 8)
            rw = kb.sbring(st, "rwB", [1, 512], F32, 4)
            yo = kb.sbring(st, "yoB", [128, 512], BF16, 2)
            ones_bb = kb.sb(st, "ones_bb", [128, 128], BF16)
            ones_ff = kb.sb(st, "ones_ff", [128, 128], F32)
            nlam = kb.sb(st, "nlam", [128, 1], F32)
            Spair = Ring([(PS[0], PS[1]), (PS[2], PS[3])])
            O = [PS[4], PS[5]]
            E0, L1 = PS[6], PS[7]
            kb.memset(ones_bb[:], 1.0, [ones_bb.r])
            kb.memset(ones_ff[:], 1.0, [ones_ff.r])
            for q_ in Q0.items:
                kb.memset(q_[64:128, :], 0.0, [q_.r], eng="pool")
            for q_ in Q1.items:
                kb.memset(q_[0:64, :], 0.0, [q_.r], eng="pool")
            kb.mm(E0[:, 0:1], ones_r[:], lam_t[:, 2:3], True, True, [ones_r.r, lam_t.r], [E0.r])
            kb.copy(nlam[:], E0[:, 0:1], [E0.r], [nlam.r])

            def load_head(h):
                k_, q0_, q1_, v_ = KT.next(), Q0.next(), Q1.next(), VV.next()
                for c in range(4):
                    cs = slice(c * 2048, (c + 1) * 2048)
                    kb.dma(k_[:, cs], kT_s[h * 128:(h + 1) * 128, cs], writes=[k_.r])
                    kb.dma(q0_[0:64, cs], qT_s[h * 128:h * 128 + 64, cs], writes=[q0_.r])
                    kb.dma(q1_[64:128, cs], qT_s[h * 128 + 64:(h + 1) * 128, cs], writes=[q1_.r])
                    kb.dma(v_[:, c * 16:(c + 1) * 16, :], v_s[h, :, c * 16:(c + 1) * 16, :], writes=[v_.r])
                return k_, (q0_, q1_), v_

            def qk_exp(k_, q_, q0, kt):
                sa, sb_ = Spair.next()
                for m, sp in enumerate((sa, sb_)):
                    kb.mm(sp[:], k_[:, kt * 128:(kt + 1) * 128], q_[m][:, q0:q0 + 512], True, True, [k_.r, q_[m].r], [sp.r])
                p_ = pt.next()
                kb.act(p_[:].rearrange("p m q -> p (m q)"), psall[:, sa.i * 512:sa.i * 512 + 1024], AF.Exp, [sa.r, sb_.r], [p_.r], scale=0.125)
                return p_

            def epilogue(h, q0, aD):
                o0, o1, l1s = f1.next(), f1.next(), f1.next()
                kb.copy(o0[:], O[0][:], [O[0].r], [o0.r])
                kb.copy(o1[:], O[1][:], [O[1].r], [o1.r])
                kb.copy(l1s[:], L1[:], [L1.r], [l1s.r])
                kb.tt(aD[0][:], aD[0][:], aD[1][:], ALU.add, [aD[0].r, aD[1].r], [aD[0].r], eng="pool")
                yield
                kb.mm(E0[:], ones_ff[:], aD[2][:], True, True, [ones_ff.r, aD[2].r], [E0.r])
                yield
                kb.tt(l1s[:], l1s[:], E0[:], ALU.add, [l1s.r, E0.r], [l1s.r])
                kb.recip(l1s[:], l1s[:], [l1s.r], [l1s.r])
                kb.stt(o1[:], o1[:], nlam[:], l1s[:], ALU.mult, ALU.mult, [o1.r, nlam.r, l1s.r], [o1.r])
                yield
                kb.mm(E0[:], ones_ff[:], aD[0][:], True, True, [ones_ff.r, aD[0].r], [E0.r])
                yield
                r0 = f1.next()
                kb.recip(r0[:], E0[:], [E0.r], [r0.r])
                kb.tt(o0[:], o0[:], r0[:], ALU.mult, [o0.r, r0.r], [o0.r])
                kb.tt(o0[:], o0[:], o1[:], ALU.add, [o0.r, o1.r], [o0.r])
                yield
                sq = f1.next()
                kb.tt(sq[:], o0[:], o0[:], ALU.mult, [o0.r], [sq.r], eng="pool")
                yield
                kb.mm(E0[0:1, :], ones_c[:], sq[:], True, True, [ones_c.r, sq.r], [E0.r])
                yield
                row = rw.next()
                kb.act(row[:], E0[0:1, :], AF.Ln, [E0.r], [row.r], scale=1.0 / 128.0, bias=RMS_EPS)
                kb.act(row[:], row[:], AF.Exp, [row.r], [row.r], scale=-0.5)
                yield
                kb.mm(E0[:], ones_r[:], row[:], True, True, [ones_r.r, row.r], [E0.r])
                yield
                yt = yo.next()
                kb.stt(yt[:], o0[:], sg_a[:], E0[:], ALU.mult, ALU.mult, [o0.r, sg_a.r, E0.r], [yt.r])
                kb.dma(yaT_s[h * 128:(h + 1) * 128, q0:q0 + 512], yt[:], reads=[yt.r])

            pend = []

            def pump():
                for g in list(pend):
                    try:
                        next(g)
                    except StopIteration:
                        pend.remove(g)

            nxt = load_head(0)
            for h in range(NHEADS_B):
                k_, q_, v_ = nxt
                if h < 3:
                    nxt = load_head(h + 1)
                steps = [(qb, kt) for qb in range(NBLK) for kt in range(64)]
                ns = len(steps)
                ptile = {}
                accs = {}

                def consume(sj):
                    qb, kt = steps[sj]
                    p_cur = ptile.pop(sj)
                    if kt == 0:
                        accs[qb] = [accD[0].next(), accD[1].next(), accD1.next()]
                    aD = accs[qb]
                    for m in range(2):
                        kb.mm(O[m][:], v_[:, kt, :], p_cur[:, m, :], kt == 0, kt == 63, [v_.r, p_cur.r], [O[m].r])
                    if kt % 3 == 2:
                        if kt == 2:
                            kb.copy(aD[2][:], p_cur[:, 1, :], [p_cur.r], [aD[2].r])
                        else:
                            kb.tt(aD[2][:], aD[2][:], p_cur[:, 1, :], ALU.add, [aD[2].r, p_cur.r], [aD[2].r])
                    else:
                        kb.mm(L1[:], ones_bb[:], p_cur[:, 1, :], kt == 0, kt == 63, [ones_bb.r, p_cur.r], [L1.r])
                    par = kt % 2
                    if kt < 2:
                        kb.copy(aD[par][:], p_cur[:, 0, :], [p_cur.r], [aD[par].r])
                    else:
                        kb.tt(aD[par][:], aD[par][:], p_cur[:, 0, :], ALU.add, [aD[par].r, p_cur.r], [aD[par].r])
                    if kt % 4 == 3:
                        pump()
                    if kt == 63:
                        g = epilogue(h, qb * 512, accs.pop(qb))
                        next(g)
                        pend.append(g)

                ptile[0] = qk_exp(k_, q_, 0, 0)
                for si in range(ns):
                    if si + 1 < ns:
                        nqb, nkt = steps[si + 1]
                        ptile[si + 1] = qk_exp(k_, q_, nqb * 512, nkt)
                    if si >= 1:
                        consume(si - 1)
                consume(ns - 1)
            while pend:
                pump()
            P.barrier()

    if "C" in phases:
        with ExitStack() as st:
            Wd = kb.sb(st, "Wd", [33, 512], F32)
            za = kb.sbring(st, "za", [33, 512], F32, 2)
            gqb = kb.sbring(st, "gqC", [128, 2, 512], F32, 2)
            gkb = kb.sbring(st, "gkC", [128, 2, 512], F32, 2)
            gktb = kb.sbring(st, "gktC", [128, 4, 256], F32, 2)
            spt = kb.sbring(st, "spt", [128, 512], F32, 3)
            et = kb.sbring(st, "etC", [128, 512], F32, 3)
            eqk = kb.sbring(st, "eqk", [128, 4, 128], F32, 3)
            ekk = kb.sbring(st, "ekk", [128, 4, 128], F32, 3)
            ekh = kb.sbring(st, "ekh", [128, 512], F32, 3)
            oq = [kb.sbring(st, f"oqC{d}", [128, 2, 512], BF16, 2) for d in range(2)]
            ok_ = [kb.sbring(st, f"okC{d}", [128, 2, 512], BF16, 2) for d in range(2)]
            oh = [kb.sbring(st, f"ohC{d}", [128, 4, 256], BF16, 2) for d in range(2)]
            wst1 = kb.sbring(st, "wstC1", [128, 2048], F32, 4)
            wsb1 = kb.sbring(st, "wsbC1", [128, 2048], BF16, 3)
            psr = Ring(PS)
            kb.memset(Wd[:], 0.0, [Wd.r])
            kb.dma(Wd[0:16, 0:256], w_dec_f, writes=[Wd.r])
            kb.dma(Wd[16:32, 256:512], w_dec_b, writes=[Wd.r])
            kb.dma(Wd[32:33, 0:256], b_dec_f, writes=[Wd.r])
            kb.dma(Wd[32:33, 256:512], b_dec_b, writes=[Wd.r])
            for z_ in za.items:
                kb.memset(z_[32:33, :], 1.0, [z_.r])

            def c1_loads(tb):
                t0 = tb * 512
                z_, gq_, gk_, gkt_ = za.next(), gqb.next(), gkb.next(), gktb.next()
                kb.dma(z_[0:32, :], zT_s[:, t0:t0 + 512], writes=[z_.r])
                kb.dma(gq_[:], gqT_s[:, t0:t0 + 512].rearrange("(h p) t -> p h t", p=128), writes=[gq_.r])
                kb.dma(gk_[:], gkT_s[:, t0:t0 + 512].rearrange("(h p) t -> p h t", p=128), writes=[gk_.r])
                kb.dma(gkt_[:], gk_s[t0:t0 + 512, :].rearrange("(n p) c -> p n c", p=128), writes=[gkt_.r])
                return z_, gq_, gk_, gkt_

            nxt = c1_loads(0)
            for tb in range(NBLK):
                t0 = tb * 512
                z_, gq_, gk_, gkt_ = nxt
                if tb + 1 < NBLK:
                    nxt = c1_loads(tb + 1)
                wj_step(wst1, wsb1, "pool", n=1)
                oq_ = [oq[d].next() for d in range(2)]
                okk = [ok_[d].next() for d in range(2)]
                ohh = [oh[d].next() for d in range(2)]
                for ti in range(4):
                    n = tb * 4 + ti
                    cs = slice(ti * 128, (ti + 1) * 128)
                    pu = psr.next()
                    kb.mm(pu[:], z_[:, cs], Wd[:], True, True, [z_.r, Wd.r], [pu.r])
                    e_, sp_ = et.next(), spt.next()
                    kb.act(e_[:], pu[:], AF.Exp, [pu.r], [e_.r], scale=-1.0)
                    kb.act(sp_[:], e_[:], AF.Ln, [e_.r], [sp_.r], bias=1.0)
                    pf, pt_ = psr.next(), psr.next()
                    for d in range(2):
                        for hp in range(2):
                            i4 = d * 2 + hp
                            kb.mm(pf[:, i4 * 128:(i4 + 1) * 128], sp_[:, d * 256 + hp * 128:d * 256 + (hp + 1) * 128], cm[:, d, :],
                                  True, True, [sp_.r, cm.r], [pf.r])
                        kb.mm(pt_[:, d * 256:(d + 1) * 256], cm[:, 2 + d, :], sp_[:, d * 256:(d + 1) * 256], True, True, [sp_.r, cm.r], [pt_.r])
                    eq_, ek_, eh_ = eqk.next(), ekk.next(), ekh.next()
                    kb.act(eq_[:].rearrange("p a t -> p (a t)"), pf[:], AF.Exp, [pf.r], [eq_.r], scale=-1.0 / 16.0)
                    kb.act(ek_[:].rearrange("p a t -> p (a t)"), pf[:], AF.Exp, [pf.r], [ek_.r], scale=1.0 / 16.0)
                    kb.act(eh_[:], pt_[:], AF.Exp, [pt_.r], [eh_.r], scale=-1.0 / 16.0)
                    kb.copy(Aall[:, 0, :, n:n + 1], eq_[:, 0:2, 127:128], [eq_.r], [Aall.r], eng="pool")
                    kb.copy(Aall[:, 1, :, n:n + 1], eq_[:, 2:4, 0:1], [eq_.r], [Aall.r], eng="pool")
                    for d in range(2):
                        kb.stt(oq_[d][:, :, cs], gq_[:, :, cs], 0.125, eq_[:, 2 * d:2 * d + 2, :], ALU.mult, ALU.mult,
                               [gq_.r, eq_.r], [oq_[d].r])
                        kb.tt(okk[d][:, :, cs], gk_[:, :, cs], ek_[:, 2 * d:2 * d + 2, :], ALU.mult, [gk_.r, ek_.r], [okk[d].r])
                        kb.tt(ohh[d][:, ti, :], gkt_[:, ti, :], eh_[:, d * 256:(d + 1) * 256], ALU.mult, [gkt_.r, eh_.r], [ohh[d].r])
                for d in range(2):
                    kb.dma(qtT_s[d][:, t0:t0 + 512].rearrange("(h p) t -> p h t", p=128), oq_[d][:], reads=[oq_[d].r])
                    kb.dma(ktT_s[d][:, t0:t0 + 512].rearrange("(h p) t -> p h t", p=128), okk[d][:], reads=[okk[d].r])
                    kb.dma(kh_s[d][t0:t0 + 512, :].rearrange("(n p) c -> p n c", p=128), ohh[d][:], reads=[ohh[d].r])
            wj_step(wst1, wsb1, "pool", n=0)
            P.barrier()

        if "G" in phases:
            with ExitStack() as st:
                qt = kb.sbring(st, "qtC", [128, 2, 512], BF16, 2)
                kt_ = kb.sbring(st, "ktC", [128, 2, 512], BF16, 2)
                kh = kb.sbring(st, "khC", [128, 4, 256], BF16, 2)
                vv = kb.sbring(st, "vvC", [128, 4, 512], BF16, 2)
                obl = kb.sbring(st, "oblC", [128, 4, 512], F32, 2)
                srl = kb.sbring(st, "srlC", [128, 4, 512], F32, 3)
                attb = kb.sbring(st, "attb", [128, 2, 128], BF16, 4)
                STf = [kb.sb(st, f"STf{hp}", [128, 256], F32) for hp in range(2)]
                STb = [kb.sbring(st, f"STb{hp}", [128, 256], BF16, 3) for hp in range(2)]
                f1 = kb.sbring(st, "f1C", [128, 512], F32, 16)
                rw = kb.sbring(st, "rwC", [1, 512], F32, 4)
                yo = kb.sbring(st, "yoC", [128, 512], BF16, 4)
                wst2 = kb.sbring(st, "wstC2", [128, 2048], F32, 4)
                wsb2 = kb.sbring(st, "wsbC2", [128, 2048], BF16, 3)
                Obank = PS[0:4]
                psr = Ring(PS[4:7])
                Ebank = PS[7]
                ones_ff2 = kb.sb(st, "ones_ff2", [128, 128], F32)
                kb.memset(ones_ff2[:], 1.0, [ones_ff2.r])

                def c2_loads(d, tb):
                    t0 = tb * 512
                    q_, k_, h_, v_ = qt.next(), kt_.next(), kh.next(), vv.next()
                    kb.dma(q_[:], qtT_s[d][:, t0:t0 + 512].rearrange("(h p) t -> p h t", p=128), writes=[q_.r])
                    kb.dma(k_[:], ktT_s[d][:, t0:t0 + 512].rearrange("(h p) t -> p h t", p=128), writes=[k_.r])
                    kb.dma(h_[:], kh_s[d][t0:t0 + 512, :].rearrange("(n p) c -> p n c", p=128), writes=[h_.r])
                    kb.dma(v_[:], gv_s[t0:t0 + 512, :].rearrange("(n p) c -> p n c", p=128), writes=[v_.r])
                    ol = sl = None
                    if d == 0:
                        ol, sl = obl.next(), srl.next()
                        kb.dma(ol[:], obT_s[:, t0:t0 + 512].rearrange("(h p) t -> p h t", p=128), writes=[ol.r])
                        kb.dma(sl[:], srT_s[:, t0:t0 + 512].rearrange("(h p) t -> p h t", p=128), writes=[sl.r])
                    return q_, k_, h_, v_, ol, sl

                def c2_epilogue(t0, ol, sl):
                    Dts = []
                    for hd in range(4):
                        Dt, sq = f1.next(), f1.next()
                        kb.tt(Dt[:], Obank[hd][:], ol[:, hd, :], ALU.add, [Obank[hd].r, ol.r], [Dt.r])
                        kb.tt(sq[:], Dt[:], Dt[:], ALU.mult, [Dt.r], [sq.r], eng="pool")
                        Dts.append((Dt, sq))
                    yield
                    for hd in range(4):
                        Dt, sq = Dts[hd]
                        kb.mm(Ebank[:], ones_ff2[:], sq[:], True, True, [ones_ff2.r, sq.r], [Ebank.r])
                        yield
                        kb.act(sq[:], Ebank[:], AF.Ln, [Ebank.r], [sq.r], scale=1.0 / 128.0, bias=RMS_EPS)
                        kb.act(sq[:], sq[:], AF.Exp, [sq.r], [sq.r], scale=-0.5)
                        yield
                        kb.stt(Dt[:], Dt[:], sg_b[:], sq[:], ALU.mult, ALU.mult, [Dt.r, sg_b.r, sq.r], [Dt.r])
                        yt = yo.next()
                        kb.tt(yt[:], Dt[:], sl[:, hd, :], ALU.mult, [Dt.r, sl.r], [yt.r], eng="pool")
                        kb.dma(ybT_s[hd * 128:(hd + 1) * 128, t0:t0 + 512], yt[:], reads=[yt.r])

                pend2 = []

                def pump2():
                    for g in list(pend2):
                        try:
                            next(g)
                        except StopIteration:
                            pend2.remove(g)

                for d in (1, 0):
                    if d == 0:
                        P.barrier()
                    cur = []
                    for hp in range(2):
                        kb.memset(STf[hp][:], 0.0, [STf[hp].r])
                        sb_ = STb[hp].next()
                        kb.memset(sb_[:], 0.0, [sb_.r])
                        cur.append(sb_)
                    blocks = list(range(NBLK)) if d == 0 else list(range(NBLK - 1, -1, -1))
                    nxt = c2_loads(d, blocks[0])
                    for bi, tb in enumerate(blocks):
                        t0 = tb * 512
                        q_, k_, h_, v_, ol, sl = nxt
                        if bi + 1 < NBLK:
                            nxt = c2_loads(d, blocks[bi + 1])
                        wj_step(wst2, wsb2, "act", n=1)
                        tiles = range(4) if d == 0 else range(3, -1, -1)
                        for ti in tiles:
                            c0 = ti * 128
                            n = tb * 4 + ti
                            abs_ = []
                            for hp in range(2):
                                ab = attb.next()
                                for hh in range(2):
                                    pa = psr.next()
                                    kb.mm(pa[:, 0:128], k_[hh * 64:(hh + 1) * 64, hp, c0:c0 + 128],
                                          q_[hh * 64:(hh + 1) * 64, hp, c0:c0 + 128], True, True, [k_.r, q_.r], [pa.r])
                                    kb.tt(ab[:, hh, :], pa[:, 0:128], msk[:, d, 0:128], ALU.mult, [pa.r, msk.r], [ab.r])
                                abs_.append(ab)
                            for hp in range(2):
                                for hh in range(2):
                                    hd = hp * 2 + hh
                                    kb.mm(Obank[hd][:, c0:c0 + 128], v_[:, ti, hd * 128:(hd + 1) * 128], abs_[hp][:, hh, :], True, False,
                                          [v_.r, abs_[hp].r], [Obank[hd].r])
                            for hp in range(2):
                                sb_ = cur[hp]
                                for hh in range(2):
                                    hd = hp * 2 + hh
                                    kb.mm(Obank[hd][:, c0:c0 + 128], sb_[hh * 64:(hh + 1) * 64, hh * 128:(hh + 1) * 128],
                                          q_[hh * 64:(hh + 1) * 64, hp, c0:c0 + 128], False, True, [sb_.r, q_.r], [Obank[hd].r])
                                pu = psr.next()
                                kb.mm(pu[:, 0:256], h_[:, ti, hp * 128:(hp + 1) * 128], v_[:, ti, hp * 256:(hp + 1) * 256],
                                      True, True, [h_.r, v_.r], [pu.r])
                                kb.stt(STf[hp][:], STf[hp][:], Aall[:, d, hp, n:n + 1], pu[:, 0:256], ALU.mult, ALU.add,
                                       [STf[hp].r, Aall.r, pu.r], [STf[hp].r])
                                nb_ = STb[hp].next()
                                kb.copy(nb_[:], STf[hp][:], [STf[hp].r], [nb_.r], eng="dve")
                                cur[hp] = nb_
                                pump2()
                                pump2()
                        for hd in range(4):
                            if d == 1:
                                of = f1.next()
                                kb.copy(of[:], Obank[hd][:], [Obank[hd].r], [of.r], eng="act")
                                kb.dma(obT_s[hd * 128:(hd + 1) * 128, t0:t0 + 512], of[:], reads=[of.r])
                        if d == 0:
                            g = c2_epilogue(t0, ol, sl)
                            next(g)
                            pend2.append(g)
                    while pend2:
                        pump2()
                while wj_state["next"] < len(wjobs) or wj_state["pend"]:
                    wj_step(wst2, wsb2, "act", n=2)
                P.barrier()

    def layer_norm_rows(h1, stats, mv, g_bc, b_bc, out_t):
        for c in range(2):
            P.op("dve", lambda e, c=c: e.bn_stats(out=stats[:, c * 6:(c + 1) * 6], in_=h1[:, c * 512:(c + 1) * 512]),
                 reads=[h1.r], writes=[stats.r])
        P.op("dve", lambda e: e.bn_aggr(out=mv[:, 0:2], in_=stats[:]), reads=[stats.r], writes=[mv.r])
        kb.act(mv[:, 2:3], mv[:, 1:2], AF.Sqrt, [mv.r], [mv.r], bias=LN_EPS)
        kb.recip(mv[:, 2:3], mv[:, 2:3], [mv.r], [mv.r])
        kb.ts(h1[:], h1[:], mv[:, 0:1], mv[:, 2:3], ALU.subtract, ALU.mult, [h1.r, mv.r], [h1.r])
        kb.tt(h1[:], h1[:], g_bc[:], ALU.mult, [h1.r, g_bc.r], [h1.r], eng="pool")
        kb.tt(out_t[:], h1[:], b_bc[:], ALU.add, [h1.r, b_bc.r], [out_t.r])

    if "D" in phases:
        with ExitStack() as st:
            wa = kb.sb(st, "wa", [128, 4, D], BF16)
            wb_ = kb.sb(st, "wb", [128, 4, D], BF16)
            wg = kb.sb(st, "wg", [128, 8, 2048], BF16)
            wo = kb.sb(st, "wo", [128, 8, D], BF16)
            bg = kb.sb(st, "bg", [128, 16], F32)
            g_bc = kb.sb(st, "g1bc", [128, D], F32)
            b_bc = kb.sb(st, "b1bc", [128, D], F32)
            ya = kb.sbring(st, "yaD", [128, 4, 512], BF16, 2)
            yb = kb.sbring(st, "ybD", [128, 4, 512], BF16, 2)
            xb = kb.sbring(st, "xbD", [128, 8, 512], BF16, 2)
            xr = kb.sbring(st, "xrD", [128, 4, D], F32, 3)
            mT = kb.sbring(st, "mT", [128, 8, 512], BF16, 2)
            f1 = kb.sbring(st, "f1D", [128, 512], F32, 6)
            stt_ = kb.sbring(st, "stD", [128, 12], F32, 2)
            mvr = kb.sbring(st, "mvD", [128, 4], F32, 2)
            psr = Ring(PS)
            def d_loads(tb):
                t0 = tb * 512
                ya_, yb_, xb_, xr_ = ya.next(), yb.next(), xb.next(), xr.next()
                kb.dma(ya_[:], yaT_s[:, t0:t0 + 512].rearrange("(h p) t -> p h t", p=128), writes=[ya_.r])
                kb.dma(yb_[:], ybT_s[:, t0:t0 + 512].rearrange("(h p) t -> p h t", p=128), writes=[yb_.r])
                kb.dma(xb_[:], xTb_s[:, t0:t0 + 512].rearrange("(c p) n -> p c n", p=128), writes=[xb_.r])
                kb.dma(xr_[:], x[t0:t0 + 512, :].rearrange("(s p) d -> p s d", p=128), writes=[xr_.r])
                return ya_, yb_, xb_, xr_

            LD = {0: d_loads(0)}
            kb.dma(wa[:], wab_s[:, :].rearrange("(c p) n -> p c n", p=128), writes=[wa.r])
            kb.dma(wb_[:], wbb_s[:, :].rearrange("(c p) n -> p c n", p=128), writes=[wb_.r])
            for c in range(2):
                kb.dma(wo[:, c * 4:(c + 1) * 4, :], wob_s[c * 512:(c + 1) * 512, :].rearrange("(c p) n -> p c n", p=128), writes=[wo.r])
            for c in range(4):
                kb.dma(wg[:, c * 2:(c + 1) * 2, :], wgb_s[c * 256:(c + 1) * 256, :].rearrange("(c p) n -> p c n", p=128), writes=[wg.r])
            kb.dma(bg[:], b_gate, writes=[bg.r])
            kb.dma(g_bc[:], ln1_g, writes=[g_bc.r])
            kb.dma(b_bc[:], ln1_b, writes=[b_bc.r])

            def d_merge(ld):
                ya_, yb_, xb_, xr_ = ld
                m_ = mT.next()
                for f in range(8):
                    pA, pB, pGa, pGb = psr.next(), psr.next(), psr.next(), psr.next()
                    for hc in range(4):
                        kb.mm(pA[:], wa[:, hc, f * 128:(f + 1) * 128], ya_[:, hc, :], hc == 0, hc == 3, [wa.r, ya_.r], [pA.r])
                    for hc in range(4):
                        kb.mm(pB[:], wb_[:, hc, f * 128:(f + 1) * 128], yb_[:, hc, :], hc == 0, hc == 3, [wb_.r, yb_.r], [pB.r])
                    for kc in range(8):
                        kb.mm(pGa[:], wg[:, kc, f * 128:(f + 1) * 128], xb_[:, kc, :], kc == 0, kc == 7, [wg.r, xb_.r], [pGa.r])
                    for kc in range(8):
                        kb.mm(pGb[:], wg[:, kc, 1024 + f * 128:1024 + (f + 1) * 128], xb_[:, kc, :], kc == 0, kc == 7, [wg.r, xb_.r], [pGb.r])
                    sa, sb2 = f1.next(), f1.next()
                    kb.act(sa[:], pGa[:], AF.Sigmoid, [pGa.r, bg.r], [sa.r], bias=bg[:, f:f + 1])
                    kb.act(sb2[:], pGb[:], AF.Sigmoid, [pGb.r, bg.r], [sb2.r], bias=bg[:, 8 + f:9 + f])
                    kb.tt(sa[:], sa[:], pA[:], ALU.mult, [sa.r, pA.r], [sa.r])
                    kb.tt(sb2[:], sb2[:], pB[:], ALU.mult, [sb2.r, pB.r], [sb2.r])
                    kb.tt(m_[:, f, :], sa[:], sb2[:], ALU.add, [sa.r, sb2.r], [m_.r], eng="pool")
                    if f % 2 == 1 and ln_jobs_d:
                        ln_jobs_d.pop(0)()
                return m_

            class RowView:
                def __init__(self, t, ts_):
                    self.t, self.ts_, self.r = t, ts_, t.r

                def __getitem__(self, idx):
                    if not isinstance(idx, tuple):
                        idx = (idx, slice(None))
                    return self.t[idx[0], self.ts_, idx[1]]

            def d_mix(tb, m_, xr_):
                t0 = tb * 512
                for ts_ in range(4):
                    for hf in range(2):
                        pm = psr.next()
                        for f in range(8):
                            kb.mm(pm[:], m_[:, f, ts_ * 128:(ts_ + 1) * 128], wo[:, f, hf * 512:(hf + 1) * 512], f == 0, f == 7,
                                  [m_.r, wo.r], [pm.r])
                        kb.stt(xr_[:, ts_, hf * 512:(hf + 1) * 512], xr_[:, ts_, hf * 512:(hf + 1) * 512], ALPHA, pm[:], ALU.mult, ALU.add,
                               [xr_.r, pm.r], [xr_.r])
                for ts_ in range(4):
                    def ln_job(v=RowView(xr_, ts_), r_=t0 + ts_ * 128):
                        layer_norm_rows(v, stt_.next(), mvr.next(), g_bc, b_bc, v)
                        kb.dma(x1_s[r_:r_ + 128, :], v[:], reads=[v.r])
                    ln_jobs_d.append(ln_job)

            ln_jobs_d = []
            LD[1] = d_loads(1)
            m_cur = d_merge(LD[0])
            for tb in range(NBLK):
                if tb + 1 < NBLK:
                    m_nxt = d_merge(LD[tb + 1])
                if tb + 2 < NBLK:
                    LD[tb + 2] = d_loads(tb + 2)
                d_mix(tb, m_cur, LD.pop(tb)[3])
                if tb + 1 < NBLK:
                    m_cur = m_nxt
            while ln_jobs_d:
                ln_jobs_d.pop(0)()
            P.barrier()

    if "E" in phases:
        with ExitStack() as st:
            w1c = [kb.sb(st, f"w1_{i}", [128, 8, 512], BF16) for i in range(8)]
            g_bc = kb.sb(st, "g2bc", [128, D], F32)
            b_bc = kb.sb(st, "b2bc", [128, D], F32)
            x1l = kb.sbring(st, "x1E", [128, D], F32, 12)
            x1T = kb.sbring(st, "x1T", [128, 8, 512], BF16, 2)
            hT = kb.sb(st, "hT", [128, 32, 512], BF16)
            w2r = kb.sbring(st, "w2E", [128, 4, D], BF16, 2)
            rl = kb.sbring(st, "rlE", [128, 512], F32, 2)
            stt_ = kb.sbring(st, "stE", [128, 12], F32, 2)
            mvr = kb.sbring(st, "mvE", [128, 4], F32, 2)
            psr = Ring(PS)

            def e_loads(tb):
                xs_ = []
                for ts_ in range(4):
                    xl = x1l.next()
                    r0 = tb * 512 + ts_ * 128
                    kb.dma(xl[:], x1_s[r0:r0 + 128, :], writes=[xl.r])
                    xs_.append(xl)
                return xs_

            def w2_load(c):
                w_ = w2r.next()
                kb.dma(w_[:], w2b_s[c * 512:(c + 1) * 512, :].rearrange("(c p) n -> p c n", p=128), writes=[w_.r])
                return w_

            def e_transposes(xls):
                xt = x1T.next()
                for ts_ in range(4):
                    for g4 in range(2):
                        ptp = psr.next()
                        for j in range(4):
                            fc = g4 * 4 + j
                            kb.tr(ptp[:, j * 128:(j + 1) * 128], xls[ts_][:, fc * 128:(fc + 1) * 128], ident, [xls[ts_].r, cm.r], [ptp.r])
                        kb.copy(xt[:, g4 * 4:(g4 + 1) * 4, ts_ * 128:(ts_ + 1) * 128], ptp[:].rearrange("p (j t) -> p j t", j=4),
                                [ptp.r], [xt.r], eng="act")
                return xt

            ln_jobs = []
            xls = e_loads(0)
            for c in range(8):
                kb.dma(w1c[c][:], w1b_s[:, c * 512:(c + 1) * 512].rearrange("(c p) n -> p c n", p=128), writes=[w1c[c].r])
            kb.dma(g_bc[:], ln2_g, writes=[g_bc.r])
            kb.dma(b_bc[:], ln2_b, writes=[b_bc.r])
            xt = e_transposes(xls)
            for tb in range(NBLK):
                t0 = tb * 512
                if tb + 1 < NBLK:
                    xls_n = e_loads(tb + 1)
                w2q = [w2_load(0), w2_load(1)]
                for ff in range(32):
                    ph = psr.next()
                    for kc in range(8):
                        kb.mm(ph[:], w1c[ff // 4][:, kc, (ff % 4) * 128:(ff % 4 + 1) * 128], xt[:, kc, :], kc == 0, kc == 7,
                              [w1c[ff // 4].r, xt.r], [ph.r])
                    r_ = rl.next()
                    kb.act(r_[:], ph[:], AF.Relu, [ph.r], [r_.r])
                    kb.tt(hT[:, ff, :], r_[:], ph[:], ALU.mult, [r_.r, ph.r], [hT.r])
                    if ff % 6 == 5 and ln_jobs:
                        ln_jobs.pop(0)()
                if tb + 1 < NBLK:
                    xt_n = e_transposes(xls_n)
                for c in range(8):
                    w_ = w2q.pop(0)
                    for ts_ in range(4):
                        for hf in range(2):
                            pm = PS[ts_ * 2 + hf]
                            for j in range(4):
                                kb.mm(pm[:], hT[:, c * 4 + j, ts_ * 128:(ts_ + 1) * 128], w_[:, j, hf * 512:(hf + 1) * 512],
                                      c == 0 and j == 0, c == 7 and j == 3, [hT.r, w_.r], [pm.r])
                    if c + 2 < 8:
                        w2q.append(w2_load(c + 2))
                for ts_ in range(4):
                    for hf in range(2):
                        pm = PS[ts_ * 2 + hf]
                        kb.stt(xls[ts_][:, hf * 512:(hf + 1) * 512], xls[ts_][:, hf * 512:(hf + 1) * 512], ALPHA, pm[:], ALU.mult, ALU.add,
                               [xls[ts_].r, pm.r], [xls[ts_].r])
                for ts_ in range(4):
                    def ln_job(xl=xls[ts_], r0=t0 + ts_ * 128):
                        layer_norm_rows(xl, stt_.next(), mvr.next(), g_bc, b_bc, xl)
                        kb.dma(y[r0:r0 + 128, :], xl[:], reads=[xl.r], is_out=True)
                    ln_jobs.append(ln_job)
                if tb + 1 < NBLK:
                    xls, xt = xls_n, xt_n
            while ln_jobs:
                ln_jobs.pop(0)()
    else:
        dummy = kb.sb(gst, "dummy_y", [128, D], F32)
        kb.memset(dummy[:], 0.0, [dummy.r])
        kb.dma(y[0:128, :], dummy[:], reads=[dummy.r], is_out=True)

    P.emit()
    gst.close()
    return nc


def _consts():
    pos = np.arange(S, dtype=np.float32)
    inv = (1.0 / (np.float32(500000.0) ** (np.arange(0, 16, 2, dtype=np.float32) / np.float32(16)))).astype(np.float32)
    ang = (pos[:, None] * inv[None, :]).astype(np.float32)
    cos, sin = np.cos(ang).astype(np.float32), np.sin(ang).astype(np.float32)
    C = np.ones((128, S), np.float32)
    Sg = np.zeros((128, S), np.float32)
    for p in range(128):
        d = p % 64
        if d < 8:
            C[p] = cos[:, d]
            Sg[p] = -sin[:, d]
        elif d < 16:
            C[p] = cos[:, d - 8]
            Sg[p] = sin[:, d - 8]
    i = np.arange(128)
    same = np.ones((128, 128), bool)
    cm = np.zeros((128, 6, 128), np.float32)
    cm[:, 0] = same & (i[:, None] <= i[None, :])
    cm[:, 1] = same & (i[:, None] >= i[None, :])
    cm[:, 2] = same & (i[:, None] > i[None, :])
    cm[:, 3] = same & (i[:, None] < i[None, :])
    cm[:, 4] = np.eye(128, dtype=np.float32)
    mk = np.zeros((128, 2, 256), np.float32)
    mf = (same & (i[:, None] <= i[None, :])).astype(np.float32)
    mb = (same & (i[:, None] >= i[None, :])).astype(np.float32)
    mk[:, 0, 0:128] = mf
    mk[:, 0, 128:256] = mf
    mk[:, 1, 0:128] = mb
    mk[:, 1, 128:256] = mb
    return C, Sg, cm, mk


def _rot_perm():
    perm = np.arange(1024)
    for c in range(1024):
        d = c % 64
        if d < 8:
            perm[c] = c + 8
        elif d < 16:
            perm[c] = c - 8
    return perm


def make_in_maps(inputs):
    f = lambda a: np.ascontiguousarray(np.asarray(a, dtype=np.float32))
    C, Sg, cm, mk = _consts()
    w_in = f(inputs["w_in"][0])
    perm = _rot_perm()
    w_rot = np.ascontiguousarray(w_in[:, :1024][:, perm])
    lamv = np.concatenate([f(inputs["lam_q1"][0]), f(inputs["lam_k1"][0]), f(inputs["lam_q2"][0]), f(inputs["lam_k2"][0])])[None, :]
    shared = {
        "w_in": w_in, "w_rot": w_rot, "ropeC": C, "ropeS": Sg, "lamv": f(lamv),
        "subln_g": f(inputs["subln_g"][0]).reshape(128, 1), "gnorm_g": f(inputs["gla_norm_g"][0]).reshape(128, 1),
        "w_dec_f": f(inputs["w_dec_f"][0]), "w_dec_b": f(inputs["w_dec_b"][0]),
        "b_dec_f": f(inputs["b_dec_f"][0]).reshape(1, 256), "b_dec_b": f(inputs["b_dec_b"][0]).reshape(1, 256),
        "w_br_a": f(inputs["w_br_a"][0]), "w_br_b": f(inputs["w_br_b"][0]),
        "b_gate": np.ascontiguousarray(f(inputs["b_gate"][0]).reshape(16, 128).T),
        "w_out": f(inputs["w_out"][0]),
        "ln1_g": np.ascontiguousarray(np.broadcast_to(f(inputs["ln1_g"][0]).reshape(1, D), (128, D))), "ln1_b": np.ascontiguousarray(np.broadcast_to(f(inputs["ln1_b"][0]).reshape(1, D), (128, D))),
        "ln2_g": np.ascontiguousarray(np.broadcast_to(f(inputs["ln2_g"][0]).reshape(1, D), (128, D))), "ln2_b": np.ascontiguousarray(np.broadcast_to(f(inputs["ln2_b"][0]).reshape(1, D), (128, D))),
        "w_mlp1": f(inputs["w_mlp1"][0]), "w_mlp2": f(inputs["w_mlp2"][0]),
        "cmats": cm, "cmask": mk,
    }
    seqs = [f(inputs["x_prompt"][i]) for i in range(4)] + [f(inputs["x_sample"][i]) for i in range(2)]
    zero = np.zeros((S, D), np.float32)
    zeroT = np.zeros((D, S), np.float32)
    maps = []
    for c in range(8):
        m = dict(shared)
        if c in CORE_OF_SEQ:
            sq = seqs[CORE_OF_SEQ.index(c)]
            m["x"] = sq
            m["xT"] = np.ascontiguousarray(sq.T)
        else:
            m["x"] = zero
            m["xT"] = zeroT
        maps.append(m)
    return maps


_NC = None


def kernel(**inputs):
    global _NC
    if _NC is None:
        _NC = build()
    maps = make_in_maps(inputs)
    res = run_bass_kernel_spmd(_NC, maps, core_ids=list(range(8)))
    outs = [np.asarray(res.results[c]["y"], dtype=np.float32) for c in CORE_OF_SEQ]
    y_prompt = np.stack(outs[0:4], axis=0)
    y_sample = np.stack(outs[4:6], axis=0)
    return (y_prompt, y_sample)
```
